# Optimizing a Trainium2 kernel written in Bass

```python
import jax, jax.numpy as jnp
from jax import lax
import numpy as np

D_MODEL = 1024
BATCH = 32
SEQ = 256
DEPTH = 1
DEC_BATCH = 4
DEC_SEQ = 1024
PAST_LEN = 256

GRID_W = 64
HEAD_DIM = 64
A_WIDTH = D_MODEL
A_HEADS = A_WIDTH // HEAD_DIM
DECAY_LORA = 64
ICLR_LORA = 64
CONV_W = 3
B_WIDTH = D_MODEL
B_Q_HEADS = B_WIDTH // HEAD_DIM
B_KV_HEADS = 4
B_GROUPS = B_Q_HEADS // B_KV_HEADS
KV_WIDTH = B_KV_HEADS * HEAD_DIM
WINDOW = 128
BLOCK = 128
KSPAN = BLOCK + 2 * WINDOW
ROPE_BASE = 10000.0
RMS_EPS = 1e-6
GN_EPS = 64e-5
NEG_INF = -1e30
ATTN_SCALE = HEAD_DIM ** -0.5
A_COLS = 3 * A_WIDTH + DECAY_LORA + ICLR_LORA
IN_SIZES = (A_COLS, A_WIDTH, B_WIDTH, KV_WIDTH, KV_WIDTH, B_WIDTH, D_MODEL, D_MODEL)
IN_WIDTH = A_COLS + A_WIDTH + 2 * B_WIDTH + 2 * KV_WIDTH + 2 * D_MODEL

kernel_name = "hybrid_rwkv7_swa_prefix_dit_step"

F32 = jnp.float32


def _split_points(sizes):
    pts, acc = [], 0
    for s in sizes[:-1]:
        acc += s
        pts.append(acc)
    return pts


def _rmsnorm(x, w):
    xf = x.astype(F32)
    y = xf * lax.rsqrt(jnp.mean(xf * xf, axis=-1, keepdims=True) + RMS_EPS)
    return (y * w).astype(x.dtype)


def _modulation(cond, w_ada, b_ada):
    m = jax.nn.silu(cond) @ w_ada + b_ada
    shift, scale, gate = jnp.split(m, 3, axis=-1)
    return shift[:, None], scale[:, None], gate[:, None]


def _short_conv(z, w):
    zp = jnp.pad(z, ((0, 0), (1, 1), (0, 0)))
    return w[0] * zp[:, :-2] + w[1] * z + w[2] * zp[:, 2:]


def _axial_rope(x):
    T = x.shape[-2]
    n_rows = T // GRID_W
    row = jnp.repeat(jnp.arange(n_rows), GRID_W)
    col = jnp.tile(jnp.arange(GRID_W), n_rows)
    half = HEAD_DIM // 2
    nf = half // 2
    inv = ROPE_BASE ** (-jnp.arange(nf, dtype=F32) / nf)

    def rot(xp, pos):
        ang = pos.astype(F32)[:, None] * inv[None, :]
        cos = jnp.cos(ang).astype(x.dtype)
        sin = jnp.sin(ang).astype(x.dtype)
        x1, x2 = xp[..., :nf], xp[..., nf:]
        return jnp.concatenate([x1 * cos - x2 * sin, x1 * sin + x2 * cos], axis=-1)

    return jnp.concatenate([rot(x[..., :half], row), rot(x[..., half:], col)], axis=-1)


def _wkv_scan(s0, r, w, k, v, a, b, reverse):
    def tm(t):
        return jnp.swapaxes(t.astype(F32), 0, 1)

    def step(s, inp):
        r_t, w_t, k_t, v_t, a_t, b_t = inp
        sa = jnp.einsum('bhvk,bhk->bhv', s, a_t)
        s = s * w_t[:, :, None, :] + sa[..., None] * b_t[:, :, None, :] + v_t[..., None] * k_t[:, :, None, :]
        return s, jnp.einsum('bhvk,bhk->bhv', s, r_t)

    s_fin, y = lax.scan(step, s0.astype(F32), (tm(r), tm(w), tm(k), tm(v), tm(a), tm(b)), reverse=reverse)
    return s_fin, jnp.swapaxes(y, 0, 1)


def _rwkv_branch(za, s0, p):
    B, T, _ = za.shape
    r, k, v, wd, ad = jnp.split(za, [A_WIDTH, 2 * A_WIDTH, 3 * A_WIDTH, 3 * A_WIDTH + DECAY_LORA], axis=-1)

    def hd(t):
        return t.reshape(B, T, A_HEADS, HEAD_DIM)

    kk = hd((k * p['k_k']).astype(F32))
    kk = kk * lax.rsqrt(jnp.maximum(jnp.sum(kk * kk, axis=-1, keepdims=True), 1e-12))
    wd_t = jnp.tanh(wd)

    def direction(d):
        w_log = -jax.nn.softplus(-(p['w0'][d] + wd_t @ p['w_up'][d]).astype(F32)) - 0.5
        decay = jnp.exp(-jnp.exp(w_log))
        a = jax.nn.sigmoid((p['a0'][d] + ad @ p['a_up'][d]).astype(F32))
        k_d = k * (1.0 + (a - 1.0) * p['k_a'])
        return _wkv_scan(s0[:, d], hd(r), hd(decay), hd(k_d), hd(v), -kk, kk * hd(a), reverse=(d == 1))

    s_fwd, y_fwd = direction(0)
    s_bwd, y_bwd = direction(1)
    y = y_fwd + y_bwd
    mu = jnp.mean(y, axis=-1, keepdims=True)
    var = jnp.mean(jnp.square(y - mu), axis=-1, keepdims=True)
    y = ((y - mu) * lax.rsqrt(var + GN_EPS)).reshape(B, T, A_WIDTH) * p['ln_x_w'] + p['ln_x_b']
    bonus = jnp.sum(hd(r) * hd(k) * p['r_k'], axis=-1, keepdims=True) * hd(v)
    y = y + bonus.reshape(B, T, A_WIDTH)
    states = jnp.stack([s_fwd, s_bwd], axis=1)
    return y.astype(za.dtype), states.astype(za.dtype)


def _heads_q(q):
    B, T, _ = q.shape
    return q.reshape(B, T, B_KV_HEADS, B_GROUPS, HEAD_DIM).transpose(0, 2, 3, 1, 4)


def _heads_kv(t):
    B, T, _ = t.shape
    return t.reshape(B, T, B_KV_HEADS, HEAD_DIM).transpose(0, 2, 1, 3)


def _merge_heads(o):
    B, _, _, T, _ = o.shape
    return o.transpose(0, 3, 1, 2, 4).reshape(B, T, B_WIDTH)


def _sink_softmax(s, sink):
    col = jnp.broadcast_to(sink.astype(F32)[None, :, :, None, None], s.shape[:-1] + (1,))
    return jax.nn.softmax(jnp.concatenate([s, col], axis=-1), axis=-1)[..., :-1]


def _context_attention(q, k, v, sink):
    B, Hk, G, T, Dh = q.shape
    nb = T // BLOCK
    qb = jnp.moveaxis(q.reshape(B, Hk, G, nb, BLOCK, Dh), 3, 0)

    def one(qi):
        s = jnp.einsum('bhgqd,bhkd->bhgqk', qi, k, preferred_element_type=F32) * ATTN_SCALE
        pr = _sink_softmax(s, sink).astype(v.dtype)
        return jnp.einsum('bhgqk,bhkd->bhgqd', pr, v)

    o = lax.map(one, qb)
    return jnp.moveaxis(o, 0, 3).reshape(B, Hk, G, T, Dh)


def _latent_attention(q, k, v, k_ctx, v_ctx, sink):
    B, Hk, G, T, Dh = q.shape
    nb = T // BLOCK
    kp = jnp.pad(k, ((0, 0), (0, 0), (WINDOW, WINDOW), (0, 0)))
    vp = jnp.pad(v, ((0, 0), (0, 0), (WINDOW, WINDOW), (0, 0)))
    qrel = jnp.arange(BLOCK)
    krel = jnp.arange(KSPAN) - WINDOW
    band = jnp.abs(qrel[:, None] - krel[None, :]) <= WINDOW

    def one(i):
        start = i * BLOCK
        qi = lax.dynamic_slice_in_dim(q, start, BLOCK, axis=3)
        ki = lax.dynamic_slice_in_dim(kp, start, KSPAN, axis=2)
        vi = lax.dynamic_slice_in_dim(vp, start, KSPAN, axis=2)
        kabs = start + krel
        mask = band & ((kabs >= 0) & (kabs < T))[None, :]
        s_loc = jnp.einsum('bhgqd,bhkd->bhgqk', qi, ki, preferred_element_type=F32) * ATTN_SCALE
        s_loc = jnp.where(mask, s_loc, NEG_INF)
        s_ctx = jnp.einsum('bhgqd,bhkd->bhgqk', qi, k_ctx, preferred_element_type=F32) * ATTN_SCALE
        pr = _sink_softmax(jnp.concatenate([s_loc, s_ctx], axis=-1), sink).astype(v.dtype)
        return (jnp.einsum('bhgqk,bhkd->bhgqd', pr[..., :KSPAN], vi)
                + jnp.einsum('bhgqk,bhkd->bhgqd', pr[..., KSPAN:], v_ctx))

    o = lax.map(one, jnp.arange(nb))
    return jnp.moveaxis(o, 0, 3).reshape(B, Hk, G, T, Dh)


def _mixer_inputs(x, cond, p):
    shift, scale, gate = _modulation(cond, p['w_ada'], p['b_ada'])
    h = _rmsnorm(x, p['norm_w']) * (1.0 + scale) + shift
    z = h @ p['w_in']
    za, ga, q, kb, vb, gb, ma, mb = jnp.split(z, _split_points(IN_SIZES), axis=-1)
    za = _short_conv(za, p['conv_a'])
    return gate, za, ga, q, kb, vb, gb, ma, mb


def _merge(x, gate, y_a, g_a, y_b, g_b, m_a, m_b, p):
    branch_a = (y_a * jax.nn.silu(g_a)) @ p['w_oA']
    branch_b = (y_b * jax.nn.silu(g_b)) @ p['w_oB']
    merged = jax.nn.sigmoid(m_a) * branch_a + jax.nn.sigmoid(m_b) * branch_b
    return x + gate * (merged @ p['w_out'])


def _context_layer(x, c_ctx, p):
    B = x.shape[0]
    gate, za, ga, q, kb, vb, gb, ma, mb = _mixer_inputs(x, c_ctx[None], p)
    s0 = jnp.zeros((B, 2, A_HEADS, HEAD_DIM, HEAD_DIM), za.dtype)
    y_a, s_ctx = _rwkv_branch(za, s0, p)
    k_c, v_c = _heads_kv(kb), _heads_kv(vb)
    y_b = _merge_heads(_context_attention(_heads_q(q), k_c, v_c, p['sink']))
    return _merge(x, gate, y_a, ga, y_b, gb, ma, mb, p), k_c, v_c, s_ctx


def _latent_layer(x, c, k_ctx, v_ctx, s_ctx, p):
    gate, za, ga, q, kb, vb, gb, ma, mb = _mixer_inputs(x, c, p)
    y_a, _ = _rwkv_branch(za, s_ctx, p)
    qh = _axial_rope(_heads_q(q))
    kh = _axial_rope(_heads_kv(kb))
    y_b = _merge_heads(_latent_attention(qh, kh, _heads_kv(vb), k_ctx, v_ctx, p['sink']))
    return _merge(x, gate, y_a, ga, y_b, gb, ma, mb, p)


def setup_inputs(seed: int = 0) -> dict:
    key = jax.random.key(seed)
    ks = jax.random.split(key, 32)

    def nrm(i, shape, s):
        return s * jax.random.normal(ks[i], shape, F32)

    L = DEPTH
    return {
        "x_prompt": nrm(0, (BATCH, SEQ, D_MODEL), 1.0),
        "x_sample": nrm(1, (DEC_BATCH, DEC_SEQ, D_MODEL), 1.0),
        "cache_k": nrm(2, (DEC_BATCH, L, B_KV_HEADS, PAST_LEN, HEAD_DIM), 1.0),
        "cache_v": nrm(3, (DEC_BATCH, L, B_KV_HEADS, PAST_LEN, HEAD_DIM), 1.0),
        "state_rwkv": nrm(4, (DEC_BATCH, L, 2, A_HEADS, HEAD_DIM, HEAD_DIM), 0.5),
        "c": nrm(5, (DEC_BATCH, D_MODEL), 1.0),
        "c_ctx": nrm(6, (D_MODEL,), 1.0),
        "norm_w": 1.0 + nrm(7, (L, D_MODEL), 0.02),
        "w_ada": nrm(8, (L, D_MODEL, 3 * D_MODEL), 0.5 * D_MODEL ** -0.5),
        "b_ada": nrm(9, (L, 3 * D_MODEL), 0.01),
        "w_in": nrm(10, (L, D_MODEL, IN_WIDTH), D_MODEL ** -0.5),
        "conv_a": jnp.array([0.0, 1.0, 0.0], F32)[None, :, None] + nrm(11, (L, CONV_W, A_COLS), 0.1),
        "w0": jax.random.uniform(ks[12], (L, 2, A_WIDTH), F32, -4.0, 1.0),
        "w_up": nrm(13, (L, 2, DECAY_LORA, A_WIDTH), 0.1),
        "a0": nrm(14, (L, 2, A_WIDTH), 0.5),
        "a_up": nrm(15, (L, 2, ICLR_LORA, A_WIDTH), 0.1),
        "k_k": 0.85 + nrm(16, (L, A_WIDTH), 0.05),
        "k_a": 1.0 + nrm(17, (L, A_WIDTH), 0.05),
        "r_k": nrm(18, (L, A_HEADS, HEAD_DIM), 0.1),
        "ln_x_w": 1.0 + nrm(19, (L, A_WIDTH), 0.02),
        "ln_x_b": nrm(20, (L, A_WIDTH), 0.01),
        "w_oA": nrm(21, (L, A_WIDTH, D_MODEL), A_WIDTH ** -0.5),
        "sink": nrm(22, (L, B_KV_HEADS, B_GROUPS), 0.5),
        "w_oB": nrm(23, (L, B_WIDTH, D_MODEL), B_WIDTH ** -0.5),
        "w_out": nrm(24, (L, D_MODEL, D_MODEL), D_MODEL ** -0.5),
        "final_norm_w": 1.0 + nrm(25, (D_MODEL,), 0.02),
    }


def reference(x_prompt, x_sample, cache_k, cache_v, state_rwkv, c, c_ctx, norm_w, w_ada, b_ada,
              w_in, conv_a, w0, w_up, a0, a_up, k_k, k_a, r_k, ln_x_w, ln_x_b, w_oA, sink, w_oB,
              w_out, final_norm_w):
    xp, xs = x_prompt, x_sample
    new_k_list, new_v_list, new_s_list = [], [], []
    for l in range(DEPTH):
        p = {
            'norm_w': norm_w[l], 'w_ada': w_ada[l], 'b_ada': b_ada[l], 'w_in': w_in[l],
            'conv_a': conv_a[l], 'w0': w0[l], 'w_up': w_up[l], 'a0': a0[l], 'a_up': a_up[l],
            'k_k': k_k[l], 'k_a': k_a[l], 'r_k': r_k[l], 'ln_x_w': ln_x_w[l], 'ln_x_b': ln_x_b[l],
            'w_oA': w_oA[l], 'sink': sink[l], 'w_oB': w_oB[l], 'w_out': w_out[l],
        }
        xp, k_c, v_c, s_c = _context_layer(xp, c_ctx, p)
        new_k_list.append(k_c)
        new_v_list.append(v_c)
        new_s_list.append(s_c)
        xs = _latent_layer(xs, c, cache_k[:, l], cache_v[:, l], state_rwkv[:, l], p)
    y_prompt = _rmsnorm(xp, final_norm_w)
    y_sample = _rmsnorm(xs, final_norm_w)
    new_k = jnp.stack(new_k_list, axis=1)
    new_v = jnp.stack(new_v_list, axis=1)
    new_state_rwkv = jnp.stack(new_s_list, axis=1)
    return (y_prompt, y_sample, new_k, new_v, new_state_rwkv)
```

```python
import numpy as np
from contextlib import ExitStack
import concourse.bass as bass
import concourse.mybir as mybir
from concourse.bass_utils import run_bass_kernel_spmd

F32 = mybir.dt.float32
BF16 = mybir.dt.bfloat16
AF = mybir.ActivationFunctionType
ALU = mybir.AluOpType
AX = mybir.AxisListType

NT = 2048
D = 1024
KAPPA = float(np.exp(-0.5))
RMS_EPS = 1e-6
GN_EPS = 64e-5
INW = 8832

PP_NORMW = 0
PP_CONV = 8
PP_KK = PP_CONV + 75
PP_KA = PP_KK + 8
PP_RK = PP_KA + 8
PP_LNW = PP_RK + 8
PP_LNB = PP_LNW + 8
PP_W0 = PP_LNB + 8
PP_A0 = PP_W0 + 16
PP_SINK = PP_A0 + 16
PP_N = PP_SINK + 16

C_IDENT = 0
C_BONES = 128
C_MX0 = 256
C_MX1 = C_MX0 + 512
C_MT0 = C_MX1 + 512
C_MT1 = C_MT0 + 256
C_TRI_GE = C_MT1 + 256
C_TRI_LE = C_TRI_GE + 128
C_PERM = C_TRI_LE + 128
C_N = C_PERM + 128


class V:
    __slots__ = ("ap", "keys")

    def __init__(self, ap, keys):
        self.ap = ap
        self.keys = list(keys) if isinstance(keys, (list, tuple)) else [keys]


class Instr:
    __slots__ = ("eng", "fn", "waits", "signal", "tick", "dma_key", "dma_tick", "seq")


class Sched:
    ENG = ["pe", "act", "dve", "pool", "sp"]

    def __init__(self):
        self.streams = {e: [] for e in self.ENG}
        self.last_w = {}
        self.readers = {}
        self.dma_counts = {}
        self.last_dma = {}
        self.nseq = 0

    def barrier(self):
        lasts = []
        for e in self.ENG:
            for ins in reversed(self.streams[e]):
                if ins.dma_key is None and ins.fn is not None:
                    lasts.append(ins)
                    break
        lasts.extend(self.last_dma.values())
        for e in self.ENG:
            b = Instr()
            b.eng, b.fn, b.waits, b.signal, b.tick, b.dma_key, b.dma_tick = e, None, [], False, 0, None, 0
            self.nseq += 1
            b.seq = self.nseq
            for d in lasts:
                if d.dma_key is None and d.eng == e:
                    continue
                b.waits.append(d)
                d.signal = True
            self.streams[e].append(b)

    def add(self, eng, fn, reads=(), writes=(), dma_key=None):
        ins = Instr()
        ins.eng = eng
        ins.fn = fn
        ins.waits = []
        ins.signal = False
        ins.tick = 0
        ins.dma_key = dma_key
        ins.dma_tick = 0
        self.nseq += 1
        ins.seq = self.nseq
        if dma_key is not None:
            self.dma_counts[dma_key] = self.dma_counts.get(dma_key, 0) + 1
            ins.dma_tick = 16 * self.dma_counts[dma_key]
            self.last_dma[dma_key] = ins
        deps = []
        for r in reads:
            w = self.last_w.get(r)
            if w is not None:
                deps.append((w, "raw"))
            if isinstance(r, str) and r.startswith("ps") and eng in ("act", "dve"):
                for q in self.readers.get(r, ()):
                    if q.eng != eng and q.eng in ("act", "dve"):
                        deps.append((q, "rar"))
        for r in writes:
            w = self.last_w.get(r)
            if w is not None:
                deps.append((w, "waw"))
            for q in self.readers.get(r, ()):
                deps.append((q, "war"))
        best = {}
        for d, kind in deps:
            if d is ins:
                continue
            if d.dma_key is None and d.eng == eng and dma_key is None:
                if eng == "pe" or kind != "raw":
                    continue
            kk_ = ("dma", d.dma_key) if d.dma_key is not None else d.eng
            cur = best.get(kk_)
            if cur is None or d.seq > cur.seq:
                best[kk_] = d
        for d in best.values():
            ins.waits.append(d)
            d.signal = True
        for r in reads:
            self.readers.setdefault(r, []).append(ins)
        for r in writes:
            self.last_w[r] = ins
            self.readers[r] = []
        self.streams[eng].append(ins)
        return ins


class KB:
    def __init__(self, nc, es, taps):
        self.nc = nc
        self.es = es
        self.s = Sched()
        self.taps = taps
        self.tap_out = {}
        self.ndma = 0

    def sb(self, name, shape, dt):
        return self.es.enter_context(self.nc.sbuf_tensor("s_s_" + name, list(shape), dt))

    def pt(self, name, shape, dt):
        return self.es.enter_context(self.nc.psum_tensor("p_" + name, list(shape), dt))

    @staticmethod
    def _rk(*ops):
        ks = []
        for o in ops:
            if isinstance(o, V):
                ks.extend(o.keys)
        return ks

    @staticmethod
    def _a(o):
        return o.ap if isinstance(o, V) else o

    def mm(self, out, lhsT, rhs, start, stop=True):
        o, l, r = out.ap, lhsT.ap, rhs.ap
        self.s.add("pe", lambda e: e.matmul(o, l, r, start=start, stop=stop),
                   self._rk(lhsT, rhs), out.keys)

    def tr(self, out, in_, ident):
        o, i, d = out.ap, in_.ap, ident.ap
        self.s.add("pe", lambda e: e.transpose(o, i, d), self._rk(in_, ident), out.keys)

    def act(self, out, in_, func, bias=0.0, scale=1.0, accum=None, eng="act"):
        o, i = out.ap, in_.ap
        b, sc = self._a(bias), self._a(scale)
        ac = self._a(accum) if accum is not None else None
        w = out.keys + (accum.keys if accum is not None else [])

        def f(e):
            if ac is not None:
                return e.activation(o, i, func, bias=b, scale=sc, accum_out=ac)
            return e.activation(o, i, func, bias=b, scale=sc)
        self.s.add(eng, f, self._rk(in_, bias, scale), w)

    def tt(self, out, in0, in1, op, eng="dve"):
        o, a, b = out.ap, in0.ap, in1.ap
        self.s.add(eng, lambda e: e.tensor_tensor(o, a, b, op), self._rk(in0, in1), out.keys)

    def ts(self, out, in0, s1, s2, op0, op1=None, accum=None, eng="dve"):
        o, a = out.ap, in0.ap
        x1, x2 = self._a(s1), self._a(s2)
        ac = self._a(accum) if accum is not None else None
        w = out.keys + (accum.keys if accum is not None else [])

        def f(e):
            kw = {}
            if op1 is not None:
                kw["op1"] = op1
            if ac is not None:
                kw["accum_out"] = ac
            return e.tensor_scalar(o, a, x1, x2, op0, **kw)
        self.s.add(eng, f, self._rk(in0, s1, s2), w)

    def stt(self, out, in0, scalar, in1, op0, op1, accum=None):
        o, a, b = out.ap, in0.ap, in1.ap
        sc = self._a(scalar)
        ac = self._a(accum) if accum is not None else None
        w = out.keys + (accum.keys if accum is not None else [])

        def f(e):
            if ac is not None:
                return e.scalar_tensor_tensor(o, a, sc, b, op0, op1, accum_out=ac)
            return e.scalar_tensor_tensor(o, a, sc, b, op0, op1)
        self.s.add("dve", f, self._rk(in0, scalar, in1), w)

    def cp(self, out, in_, eng="dve"):
        o, i = out.ap, in_.ap
        if eng == "act":
            self.s.add("act", lambda e: e.copy(o, i), in_.keys, out.keys)
        else:
            self.s.add(eng, lambda e: e.tensor_scalar(o, i, 1.0, None, ALU.mult), in_.keys, out.keys)

    def rsum(self, out, in_):
        o, i = out.ap, in_.ap
        self.s.add("dve", lambda e: e.reduce_sum(o, i, AX.X), in_.keys, out.keys)

    def recip(self, out, in_):
        o, i = out.ap, in_.ap
        self.s.add("dve", lambda e: e.reciprocal(o, i), in_.keys, out.keys)

    def scan(self, out, d0, d1, init, op0, op1):
        o, a, b = out.ap, d0.ap, d1.ap
        self.s.add("dve", lambda e: e.tensor_tensor_scan(o, a, b, init, op0, op1),
                   self._rk(d0, d1), out.keys)

    def memset(self, out, val, eng="dve"):
        o = out.ap
        self.s.add(eng, lambda e: e.memset(o, val), [], out.keys)

    def dma(self, out, in_, key=None, eng="sp", **kw):
        o, i = out.ap, in_.ap
        if key is None:
            key = "d%d" % self.ndma
        self.ndma += 1
        if eng == "pool":
            kw.setdefault("max_dma_last_dim", 2048)
        self.s.add(eng, lambda e: e.dma_start(out=o, in_=i, **kw), in_.keys, out.keys, dma_key=key)

    def tap(self, name, v, dt=F32):
        if name not in self.taps:
            return
        shape = list(v.ap.shape)
        t = self.nc.dram_tensor("tap_" + name, shape, dt, kind="ExternalOutput").ap()
        self.tap_out[name] = shape
        self.dma(V(t, "tapdram_" + name), v, key="tap")

    def emit(self):
        nc, s = self.nc, self.s
        es = self.es
        engsem = {e: es.enter_context(nc.semaphore("sem_" + e)) for e in Sched.ENG}
        dmasem = {k: es.enter_context(nc.semaphore("dsem_%d" % i)) for i, k in enumerate(s.dma_counts)}
        for e in Sched.ENG:
            c = 0
            for ins in s.streams[e]:
                if ins.dma_key is None and ins.signal and ins.fn is not None:
                    c += 1
                    ins.tick = c
        block = es.enter_context(nc.Block())
        engobj = {"pe": block.tensor, "act": block.scalar, "dve": block.vector, "pool": block.gpsimd,
                  "sp": block.sync}

        def mk(ename):
            def body(e):
                waited = {}
                for ins in s.streams[ename]:
                    for d in ins.waits:
                        if d.dma_key is not None:
                            k, val, sem = ("dma", d.dma_key), d.dma_tick, dmasem[d.dma_key]
                        else:
                            k, val, sem = d.eng, d.tick, engsem[d.eng]
                        if waited.get(k, 0) >= val:
                            continue
                        e.wait_ge(sem, val)
                        waited[k] = val
                    if ins.fn is None:
                        continue
                    bi = ins.fn(e)
                    if ins.dma_key is not None:
                        bi.then_inc(dmasem[ins.dma_key], 16)
                    elif ins.signal:
                        bi.then_inc(engsem[ename], 1)
                if ename == "sp":
                    for k, cnt in s.dma_counts.items():
                        e.wait_ge(dmasem[k], 16 * cnt)
            return body
        for ename in Sched.ENG:
            engobj[ename](mk(ename))


class StopBuild(Exception):
    pass


def build(taps=(), phases=("p0", "p1", "rwkv", "attn", "merge"), stop=None):
    def ck(name):
        if stop == name:
            raise StopBuild()
    nc = bass.Bass("TRN2", target_bir_lowering=False)
    es = ExitStack()
    k = KB(nc, es, set(taps))

    def din(name, shape):
        return nc.dram_tensor(name, list(shape), F32, kind="ExternalInput").ap()

    def dout(name, shape):
        return nc.dram_tensor(name, list(shape), F32, kind="ExternalOutput").ap()

    x_d = din("x", [NT, D])
    condT_d = din("condT", [128, 16])
    wada_d = din("w_ada", [D, 3 * D])
    bada_d = din("bada2", [2, 3 * D])
    win_d = din("w_in", [D, INW])
    pp_d = din("pp", [128, PP_N])
    lora_d = din("lora", [128, 2, 1024])
    st0_d = din("st0", [8, 2, 128, 64])
    cst_d = din("cst", [128, C_N])
    rope_d = din("rope", [2, 128, 1024])
    cachek_d = din("cache_k", [4, 256, 64])
    cachev_d = din("cache_v", [4, 256, 64])
    woa_d = din("w_oA", [D, D])
    wob_d = din("w_oB", [D, D])
    wout_d = din("w_out", [D, D])
    fnw_d = din("fnw", [128, D])
    newk_d = dout("newk", [4, 4, 256, 64])
    newv_d = dout("newv", [4, 4, 256, 64])
    y_d = dout("y", [NT, D])
    newsT_d = dout("newsT", [4, 2, 8, 128, 64])

    win_v = win_d.rearrange("(kc p) n -> p kc n", p=128)

    cst = k.sb("cst", [128, 256], F32)
    pp = k.sb("pp", [128, PP_N + 160], F32)
    PP_NEGW0 = PP_N
    PP_NEGW2 = PP_N + 25
    PP_OMKA = PP_N + 50
    PP_GNEPS = PP_N + 58
    k.dma(V(cst[:], "cst"), V(cst_d[:, 0:256], "cst_d"), key="c1")
    k.dma(V(pp[:, 0:PP_N], "pp"), V(pp_d, "pp_d"), key="c2")
    cstb = k.sb("cstb", [128, C_N], BF16)
    k.dma(V(cstb[:], "cstb"), V(cst_d, "cst_d"), key="c3", eng="pool")
    convv = pp[:, PP_CONV:PP_CONV + 75].rearrange("p (c j) -> p c j", j=3)
    k.ts(V(pp[:, PP_NEGW0:PP_NEGW0 + 25], "pp2"), V(convv[:, :, 0], "pp"), -1.0, None, ALU.mult)
    k.ts(V(pp[:, PP_NEGW2:PP_NEGW2 + 25], "pp2"), V(convv[:, :, 2], "pp"), -1.0, None, ALU.mult)
    k.ts(V(pp[:, PP_OMKA:PP_OMKA + 8], "pp2"), V(pp[:, PP_KA:PP_KA + 8], "pp"), -1.0, 1.0, ALU.mult, ALU.add)
    k.memset(V(pp[:, PP_GNEPS:PP_GNEPS + 1], "pp2"), GN_EPS)
    PPK = ["pp", "pp2"]

    def ppc(col):
        return V(pp[:, col:col + 1], PPK)

    identf = V(cst[:, C_IDENT:C_IDENT + 128], "cst")
    identb = V(cstb[:, C_IDENT:C_IDENT + 128], "cstb")
    bones = V(cst[:, C_BONES:C_BONES + 128], "cst")

    PS = [k.pt("ps%d" % i, [128, 512], F32) for i in range(7)]
    PSB = k.pt("psb", [128, 1024], BF16)

    hT = k.sb("hT", [128, 8, NT], BF16)

    def hTk(tb):
        return ["hT%d" % (4 * tb + i) for i in range(4)]

    mT = k.sb("mT", [128, 48], F32)
    gmod = k.sb("gmod", [128, 8, 2], F32)
    gate_bc = k.sb("gate_bc", [128, 2, D], BF16)
    with ExitStack() as es0:
        sc = es0.enter_context(nc.sbuf_tensor("s_sc", [128, 16], F32))
        wada = [es0.enter_context(nc.sbuf_tensor("s_wada%d" % i, [128, 8, 512], F32)) for i in range(2)]
        m_sb = es0.enter_context(nc.sbuf_tensor("s_m_sb", [2, 3 * D], F32))
        bada = es0.enter_context(nc.sbuf_tensor("s_bada", [2, 3 * D], F32))
        sel = es0.enter_context(nc.sbuf_tensor("s_sel", [2, 2, 128], F32))
        k.dma(V(sc[:], "sc"), V(condT_d, "condT_d"), key="c4")
        k.dma(V(bada[:], "bada"), V(bada_d, "bada_d"), key="c5")
        k.act(V(sc[:], "sc"), V(sc[:], "sc"), AF.Silu)
        scv = sc[:].rearrange("p (c r) -> p c r", r=2)
        wada_v = wada_d.rearrange("(kc p) n -> p kc n", p=128)
        for blk in range(6):
            sl = blk % 2
            k.dma(V(wada[sl][:], "wada%d" % sl), V(wada_v[:, :, blk * 512:(blk + 1) * 512], "wada_d"),
                  key="wada%d" % sl)
            for kc in range(8):
                k.mm(V(PS[0][0:2, :], "ps0"), V(scv[:, kc, :], "sc"), V(wada[sl][:, kc, :], "wada%d" % sl),
                     start=(kc == 0), stop=(kc == 7))
            k.tt(V(m_sb[:, blk * 512:(blk + 1) * 512], "m_sb"), V(PS[0][0:2, :], "ps0"),
                 V(bada[:, blk * 512:(blk + 1) * 512], "bada"), ALU.add)
        for j in range(24):
            k.tr(V(PS[1][:, 2 * j:2 * j + 2], "ps1"), V(m_sb[0:2, j * 128:(j + 1) * 128], "m_sb"),
                 V(cst[0:2, C_IDENT:C_IDENT + 2], "cst"))
        k.cp(V(mT[:], "mT"), V(PS[1][:, 0:48], "ps1"))
        k.ts(V(gmod[:], "gmod"), V(mT[:, 16:32].rearrange("p (c r) -> p c r", r=2), "mT"), 1.0, None, ALU.add)
        k.tt(V(gmod[:], "gmod"), V(gmod[:], "gmod"),
             V(pp[:, PP_NORMW:PP_NORMW + 8].unsqueeze(2).to_broadcast([128, 8, 2]), "pp"), ALU.mult)
        k.memset(V(sel[:], "sel"), 0.0)
        k.memset(V(sel[0:1, 0, :], "sel"), 1.0)
        k.ts(V(sel[:, 1, :], "sel"), V(sel[:, 0, :], "sel"), -1.0, 1.0, ALU.mult, ALU.add)
        for r in range(2):
            for hb in range(2):
                k.mm(V(PS[2][:, :], "ps2"), V(sel[:, r, :], "sel"),
                     V(m_sb[0:2, 2048 + hb * 512:2048 + (hb + 1) * 512], "m_sb"), start=True)
                k.cp(V(gate_bc[:, r, hb * 512:(hb + 1) * 512], "gate_bc"), V(PS[2][:, :], "ps2"))
        k.tap("mT", V(mT[:], "mT"))
        k.tap("gate_bc", V(gate_bc[:], "gate_bc"), BF16)
        k.s.barrier()

    with ExitStack() as es1:
        xall = es1.enter_context(nc.sbuf_tensor("s_xall", [128, 16, D], F32))
        junk = es1.enter_context(nc.sbuf_tensor("s_junk", [128, D], F32))
        ss = es1.enter_context(nc.sbuf_tensor("s_ss", [128, 16], F32))
        rstd = es1.enter_context(nc.sbuf_tensor("s_rstd", [128, 16], F32))
        for tt_ in range(16):
            k.dma(V(xall[:, tt_, :], "xall%d" % tt_), V(x_d[tt_ * 128:(tt_ + 1) * 128, :], "x_d"),
                  key="x%d" % tt_)
            k.tt(V(junk[:], "junk"), V(xall[:, tt_, :], "xall%d" % tt_), V(xall[:, tt_, :], "xall%d" % tt_), ALU.mult)
            k.rsum(V(ss[:, tt_:tt_ + 1], "ss"), V(junk[:], "junk"))
        k.act(V(rstd[:], "rstd"), V(ss[:], "ss"), AF.Sqrt, bias=RMS_EPS, scale=1.0 / D)
        k.recip(V(rstd[:], "rstd"), V(rstd[:], "rstd"))
        for tt_ in range(16):
            r = 0 if tt_ < 8 else 1
            k.act(V(xall[:, tt_, :], "xall%d" % tt_), V(xall[:, tt_, :], "xall%d" % tt_), AF.Copy,
                  scale=V(rstd[:, tt_:tt_ + 1], "rstd"))
            for half in range(2):
                bank = PS[3 + half]
                bk = "ps%d" % (3 + half)
                for q in range(4):
                    kc = half * 4 + q
                    k.tr(V(bank[:, q * 128:(q + 1) * 128], bk), V(xall[:, tt_, kc * 128:(kc + 1) * 128], "xall%d" % tt_),
                         identf)
                for q in range(4):
                    kc = half * 4 + q
                    k.ts(V(hT[:, kc, tt_ * 128:(tt_ + 1) * 128], "hT%d" % tt_), V(bank[:, q * 128:(q + 1) * 128], bk),
                         V(gmod[:, kc, r:r + 1], "gmod"), V(mT[:, 2 * kc + r:2 * kc + r + 1], "mT"),
                         ALU.mult, ALU.add)
        k.tap("hT", V(hT[:], ["hT%d" % i for i in range(16)]), BF16)
        k.s.barrier()


    UaT = k.sb("UaT", [128, 8, NT], BF16)
    NTH = 1024
    def rwkv_phase():
        es2 = ExitStack()

        def sbl(name, shape, dt):
            return es2.enter_context(nc.sbuf_tensor("s_" + name, list(shape), dt))
        NW = 6
        wsl = [sbl("wsl%d" % i, [128, 8, 128], BF16) for i in range(NW)]
        wcount = [0]

        def load_w(col0):
            sl = wcount[0] % NW
            wcount[0] += 1
            k.dma(V(wsl[sl][:], "wsl%d" % sl), V(win_v[:, :, col0:col0 + 128], "win_d"), key="wsl%d" % sl, eng="pool")
            return sl

        def proj(sl, tbs, evac):
            for tb in tbs:
                bank = tb % 2
                for kc in range(8):
                    k.mm(V(PS[bank][:, :], "ps%d" % bank), V(wsl[sl][:, kc, :], "wsl%d" % sl),
                         V(hT[:, kc, tb * 512:(tb + 1) * 512], hTk(tb)), start=(kc == 0), stop=(kc == 7))
                evac(tb, V(PS[bank][:, :], "ps%d" % bank))

        def conv_fix(z0, zk, dst, dk, ch, n, bounds):
            w0, w2 = ppc(PP_CONV + ch * 3 + 0), ppc(PP_CONV + ch * 3 + 2)
            k.stt(V(dst[:, 1:n], dk), V(z0[:, 0:n - 1], zk), w0, V(dst[:, 1:n], dk), ALU.mult, ALU.add)
            k.stt(V(dst[:, 0:n - 1], dk), V(z0[:, 1:n], zk), w2, V(dst[:, 0:n - 1], dk), ALU.mult, ALU.add)
            if bounds:
                b0, b1, st = bounds
                k.stt(V(dst[:, b0:b1:st], dk), V(z0[:, b0 - 1:b1 - 1:st], zk), ppc(PP_NEGW0 + ch),
                      V(dst[:, b0:b1:st], dk), ALU.mult, ALU.add)
                k.stt(V(dst[:, b0 - 1:b1 - 1:st], dk), V(z0[:, b0:b1:st], zk), ppc(PP_NEGW2 + ch),
                      V(dst[:, b0 - 1:b1 - 1:st], dk), ALU.mult, ALU.add)

        WA = sbl("WA", [128, NT], BF16)
        ck("pre_wa")
        LW = sbl("LW", [128, 2, 1024], BF16)
        k.dma(V(LW[:], "LW"), V(lora_d, "lora_d"), key="c6", eng="pool")
        Z0f = sbl("Z0f", [128, NT], F32)
        Z1f = sbl("Z1f", [128, NT], F32)
        slw = load_w(3072)

        def ev_wa(tb, ps):
            k.act(V(Z0f[:, tb * 512:(tb + 1) * 512], "Z0f"), ps, AF.Copy)
            k.act(V(Z1f[:, tb * 512:(tb + 1) * 512], "Z1f"), ps, AF.Copy, scale=ppc(PP_CONV + 24 * 3 + 1))
        proj(slw, range(4), ev_wa)
        conv_fix(Z0f, "Z0f", Z1f, "Z1f", 24, NT, (256, 1280, 256))
        k.act(V(WA[0:64, :], "WA"), V(Z1f[0:64, :], "Z1f"), AF.Tanh)
        k.act(V(WA[64:128, :], "WA"), V(Z1f[64:128, :], "Z1f"), AF.Copy)
        k.tap("WA", V(WA[:], "WA"), BF16)
        ck("wa")

        Z0 = Z0f
        YACC = Z1f
        RC = sbl("RC", [128, NTH], F32)
        KC = sbl("KC", [128, NTH], F32)
        VC = sbl("VC", [128, NTH], BF16)
        KK = sbl("KK", [128, NTH], F32)
        BON = sbl("BON", [128, NTH], BF16)
        AR = sbl("AR", [128, 8, 2, 128], BF16)
        BKt = sbl("BKt", [128, 8, 2, 128], BF16)
        BHT = sbl("BHT", [128, 8, 128], BF16)
        KHT = sbl("KHT", [128, 8, 128], BF16)
        VT = sbl("VT", [128, 8, 128], BF16)
        PCt = sbl("PCt", [128, 8], F32)
        RM = sbl("RM", [128, 512], BF16)
        k.memset(V(RM[:], "RM"), 1.0)
        k.memset(V(RM[:, 0:512:128], "RM"), 0.0)
        tn = ["SG", "AA", "CUM", "QB", "QC", "QD", "E1", "E2", "E3", "E4", "BA", "KD", "TMP"]
        T = {n: sbl("t_" + n, [128, 512], F32) for n in tn}
        BH = sbl("BH", [128, 512], BF16)
        KH = sbl("KH", [128, 512], BF16)
        NSET = 4
        GQ = 2
        AM = [[sbl("AM%d_%d" % (g, e), [128, 512], BF16) for e in range(2)] for g in range(NSET)]
        NTs = [sbl("NTs%d" % g, [128, 256], BF16) for g in range(NSET)]
        N2s = [sbl("N2s%d" % g, [128, 512], BF16) for g in range(NSET)]
        Zb = [sbl("Zb%d" % g, [128, 256], BF16) for g in range(NSET)]
        WTs = [sbl("WTs%d" % g, [128, 2, 64], F32) for g in range(NSET)]
        Ub = [sbl("Ub%d" % g, [128, 2, 64], BF16) for g in range(NSET)]
        UT = [sbl("UT%d" % g, [128, 128], BF16) for g in range(NSET)]
        SAT = [sbl("SAT%d" % g, [128, 128], BF16) for g in range(2)]
        STs = [sbl("ST%d" % g, [128, 64], F32) for g in range(2)]
        STb = sbl("STb", [128, 128], BF16)
        k.memset(V(STb[:], "STb"), 0.0)
        stcount = [0]
        stepcount = [0]

        def vt(name, lo=0, hi=512):
            return V(T[name][:, lo:hi], "t_" + name)

        def v3(name):
            return T[name][:].rearrange("p (n t) -> p n t", t=128)

        for c in range(8):
            sl_r, sl_k, sl_v, sl_g = load_w(c * 128), load_w(1024 + c * 128), load_w(2048 + c * 128), \
                load_w(3200 + c * 128)
            for half in range(2):
                t0 = half * NTH
                tbs = [2 * half, 2 * half + 1]
                bounds = (256, 1024, 256) if half == 0 else None
                for (slq, dst, dk, ch) in ((sl_r, RC, "RC", c), (sl_k, KC, "KC", 8 + c), (sl_v, YACC, ["YACC0", "YACC1"], 16 + c)):
                    def ev(tb, ps, dst=dst, dk=dk, ch=ch):
                        lo = (tb - 2 * half) * 512
                        k.act(V(Z0[:, lo:lo + 512], "Z0f"), ps, AF.Copy)
                        k.act(V(dst[:, lo:lo + 512], dk), ps, AF.Copy, scale=ppc(PP_CONV + ch * 3 + 1))
                    proj(slq, tbs, ev)
                    conv_fix(Z0, "Z0f", dst, dk, ch, NTH, bounds)
                k.act(V(VC[:], "VC"), V(YACC[:, 0:NTH], ["YACC0", "YACC1"]), AF.Copy)
                ck("conv")
                if c == 0:
                    k.tap("RC%d" % half, V(RC[:], "RC"))
                    k.tap("KC%d" % half, V(KC[:], "KC"))
                    k.tap("VC%d" % half, V(VC[:], "VC"), BF16)
                def kk_block(j, tA, tB, tC):
                    lo = j * 512
                    k.ts(vt(tA), V(KC[:, lo:lo + 512], "KC"), ppc(PP_KK + c), None, ALU.mult)
                    yield
                    k.tt(vt(tB), vt(tA), vt(tA), ALU.mult)
                    yield
                    k.mm(V(PS[4 + j][:, :], "ps%d" % (4 + j)), bones, vt(tB), start=True)
                    k.stt(vt(tC), V(RC[:, lo:lo + 512], "RC"), ppc(PP_RK + c), V(KC[:, lo:lo + 512], "KC"),
                          ALU.mult, ALU.mult)
                    yield
                    k.ts(vt(tB), V(PS[4 + j][:, :], "ps%d" % (4 + j)), 1e-12, None, ALU.max)
                    k.mm(V(PS[2 + j][:, :], "ps%d" % (2 + j)), bones, vt(tC), start=True)
                    yield
                    k.act(vt(tB), vt(tB), AF.Ln)
                    yield
                    k.act(vt(tB), vt(tB), AF.Exp, scale=-0.5)
                    k.tt(V(BON[:, lo:lo + 512], "BON"), V(PS[2 + j][:, :], "ps%d" % (2 + j)), V(VC[:, lo:lo + 512], "VC"),
                         ALU.mult)
                    yield
                    k.tt(V(KK[:, lo:lo + 512], "KK"), vt(tA), vt(tB), ALU.mult)
                gens = [kk_block(0, "TMP", "E1", "E2"), kk_block(1, "SG", "AA", "CUM")]
                while gens:
                    for g_ in list(gens):
                        try:
                            next(g_)
                        except StopIteration:
                            gens.remove(g_)
                for n in range(8):
                    k.tr(V(PSB[:, n * 128:(n + 1) * 128], "psb"), V(VC[:, n * 128:(n + 1) * 128], "VC"), identb)
                k.act(V(VT[:].rearrange("p n c -> p (n c)"), "VT"), V(PSB[:, :], "psb"), AF.Copy)
                if c == 0:
                    k.tap("KK%d" % half, V(KK[:], "KK"))
                ck("kk")

                for d in range(2):
                    SGs = [vt("SG"), V(Z0f[:, 0:512], "Z0f")]
                    AAs = [vt("AA"), V(Z0f[:, 512:1024], "Z0f")]
                    for j in range(2):
                        g0 = t0 + j * 512
                        pa_, pb_ = (2, 3) if j == 0 else (4, 5)
                        k.mm(V(PS[pa_][:, :], "ps%d" % pa_), V(LW[0:64, d, c * 128:(c + 1) * 128], "LW"),
                             V(WA[0:64, g0:g0 + 512], "WA"), start=True)
                        k.mm(V(PS[pb_][:, :], "ps%d" % pb_), V(LW[64:128, d, c * 128:(c + 1) * 128], "LW"),
                             V(WA[64:128, g0:g0 + 512], "WA"), start=True)
                        k.act(SGs[j], V(PS[pa_][:, :], "ps%d" % pa_), AF.Sigmoid, bias=ppc(PP_W0 + d * 8 + c))
                        k.act(AAs[j], V(PS[pb_][:, :], "ps%d" % pb_), AF.Sigmoid, bias=ppc(PP_A0 + d * 8 + c))
                    for j in range(2):
                        lo = j * 512
                        g0 = t0 + lo
                        SGv, AAv = SGs[j], AAs[j]
                        k.scan(vt("CUM"), V(RM[:, :], "RM"), SGv, 0.0, ALU.mult, ALU.add)
                        tot = V(v3("CUM")[:, :, 127:128].to_broadcast([128, 4, 128]), "t_CUM")
                        k.tt(vt("QB"), vt("CUM"), SGv, ALU.subtract)
                        k.tt(V(v3("QC"), "t_QC"), tot, V(v3("CUM"), "t_CUM"), ALU.subtract)
                        if d == 0:
                            e_in, e_ex, e_rem = "CUM", "QB", "QC"
                        else:
                            k.tt(vt("QD"), vt("QC"), SGv, ALU.add)
                            e_in, e_ex, e_rem = "QD", "QC", "QB"
                        k.act(vt("E1"), vt(e_in), AF.Exp, scale=-KAPPA)
                        k.act(vt("E2"), vt(e_ex), AF.Exp, scale=-KAPPA)
                        k.act(vt("E3"), vt(e_in), AF.Exp, scale=KAPPA)
                        k.act(vt("E4"), vt(e_rem), AF.Exp, scale=-KAPPA)
                        k.act(V(PCt[:, 4 * j:4 * j + 4], "PCt"), V(T["CUM"][:, 127:512:128], "t_CUM"), AF.Exp,
                              scale=-KAPPA)
                        k.tt(vt("BA"), V(KK[:, lo:lo + 512], "KK"), AAv, ALU.mult)
                        k.ts(vt("TMP"), AAv, ppc(PP_KA + c), ppc(PP_OMKA + c), ALU.mult, ALU.add)
                        k.tt(vt("KD"), V(KC[:, lo:lo + 512], "KC"), vt("TMP"), ALU.mult)
                        ns = slice(4 * j, 4 * j + 4)
                        k.stt(V(AR[:, ns, 0, :], "AR"), V(KK[:, lo:lo + 512].rearrange("p (n t) -> p n t", t=128), "KK"),
                              -1.0, V(v3("E2"), "t_E2"), ALU.mult, ALU.mult)
                        k.tt(V(AR[:, ns, 1, :], "AR"), V(RC[:, lo:lo + 512].rearrange("p (n t) -> p n t", t=128), "RC"),
                             V(v3("E1"), "t_E1"), ALU.mult)
                        k.tt(V(BKt[:, ns, 0, :], "BKt"), V(v3("BA"), "t_BA"), V(v3("E3"), "t_E3"), ALU.mult)
                        k.tt(V(BKt[:, ns, 1, :], "BKt"), V(v3("KD"), "t_KD"), V(v3("E3"), "t_E3"), ALU.mult)
                        k.tt(V(BH[:], "BH"), vt("BA"), vt("E4"), ALU.mult)
                        k.tt(V(KH[:], "KH"), vt("KD"), vt("E4"), ALU.mult)
                        for n in range(4):
                            k.tr(V(PSB[:, n * 128:(n + 1) * 128], "psb"), V(BH[:, n * 128:(n + 1) * 128], "BH"), identb)
                            k.tr(V(PSB[:, 512 + n * 128:512 + (n + 1) * 128], "psb"),
                                 V(KH[:, n * 128:(n + 1) * 128], "KH"), identb)
                        k.act(V(BHT[:, ns, :].rearrange("p n c -> p (n c)"), "BHT"), V(PSB[:, 0:512], "psb"), AF.Copy)
                        k.act(V(KHT[:, ns, :].rearrange("p n c -> p (n c)"), "KHT"), V(PSB[:, 512:1024], "psb"), AF.Copy)
                        if c == 0 and half == 0 and j == 0:
                            k.tap("SG%d" % d, SGv)
                            k.tap("AA%d" % d, AAv)
                            k.tap("CUM%d" % d, vt("CUM"))
                    if c == 0:
                        k.tap("AR%d%d" % (half, d), V(AR[:], "AR"), BF16)
                        k.tap("BKt%d%d" % (half, d), V(BKt[:], "BKt"), BF16)
                        k.tap("BHT%d%d" % (half, d), V(BHT[:], "BHT"), BF16)
                        k.tap("PCt%d%d" % (half, d), V(PCt[:], "PCt"))

                    ck("prep")
                    if half == 0:
                        seqs = [[2 * q, 2 * q + 1] for q in range(4)]
                    else:
                        seqs = [list(range(8))]
                    items = []
                    for qi, chs in enumerate(seqs):
                        if d == 1:
                            chs = chs[::-1]
                        for ii, n in enumerate(chs):
                            items.append((n, qi, ii == 0, ii == len(chs) - 1))
                    mxk = C_MX0 if d == 0 else C_MX1
                    mtk = C_MT0 if d == 0 else C_MT1
                    maskx = V(cstb[:, mxk:mxk + 512], "cstb")
                    maskt = V(cstb[:, mtk:mtk + 128], "cstb")

                    def pre_pair(pi):
                        its = items[GQ * pi:GQ * pi + GQ]
                        gs = [(GQ * pi + q) % NSET for q in range(GQ)]
                        for q, (n, _, _, _) in enumerate(its):
                            g = gs[q]
                            n2b, zb_, zc = 2 + q, 4 + q, 0
                            for e in range(2):
                                pr = slice(64 * e, 64 * e + 64)
                                pk = "ps%d" % e
                                k.mm(V(PS[e][:, 0:256], pk), V(BKt[pr, n, 0, :], "BKt"),
                                     V(AR[pr, n, :, :].rearrange("p a t -> p (a t)"), "AR"), start=True)
                                k.mm(V(PS[e][:, 256:512], pk), V(BKt[pr, n, 1, :], "BKt"),
                                     V(AR[pr, n, :, :].rearrange("p a t -> p (a t)"), "AR"), start=False)
                                nb_ = n2b if e == 0 else zb_
                                ntb = PS[nb_]
                                ntk = "ps%d" % nb_
                                ntc = 0 if e == 0 else 256
                                k.mm(V(ntb[:, ntc:ntc + 128], ntk), V(AR[pr, n, 0, :], "AR"), V(BKt[pr, n, 0, :], "BKt"),
                                     start=True)
                                k.tt(V(AM[g][e][:], "AM%d_%d" % (g, e)), V(PS[e][:, :], pk), maskx, ALU.mult)
                                k.tt(V(NTs[g][:, e * 128:(e + 1) * 128], "NTs%d" % g), V(ntb[:, ntc:ntc + 128], ntk),
                                     maskt, ALU.mult)
                            ck("pa")
                            zk_ = "ps%d" % zb_
                            for e in range(2):
                                k.mm(V(PS[zb_][:, zc + e * 128:zc + e * 128 + 64], zk_), V(AM[g][e][:, 256:384], "AM%d_%d" % (g, e)),
                                     V(VT[:, n, 64 * e:64 * e + 64], "VT"), start=(e == 0))
                                k.mm(V(PS[zb_][:, zc + e * 128 + 64:zc + (e + 1) * 128], zk_), V(AR[:, n, 0, :], "AR"),
                                     V(cstb[:, C_IDENT + 64 * e:C_IDENT + 64 * e + 64], "cstb"), start=False)
                        ck("pz")
                        yield
                        ncur = [[V(AM[gs[q]][e][:, 0:128], "AM%d_%d" % (gs[q], e)) for e in range(2)] for q in range(GQ)]
                        ntcur = [[V(NTs[gs[q]][:, e * 128:(e + 1) * 128], "NTs%d" % gs[q]) for e in range(2)]
                                 for q in range(GQ)]
                        for lev in range(7):
                            for q in range(len(its)):
                                g = gs[q]
                                n2b, zb_, zc = 2 + q, 4 + q, 0
                                zk_ = "ps%d" % zb_
                                k.act(V(Zb[g][:], "Zb%d" % g), V(PS[zb_][:, 0:256], zk_), AF.Copy)
                                for e in range(2):
                                    k.mm(V(PS[zb_][:, e * 128:(e + 1) * 128], zk_), ncur[q][e],
                                         V(Zb[g][:, e * 128:(e + 1) * 128], "Zb%d" % g), start=False)
                                if lev < 6:
                                    nk_ = "ps%d" % n2b
                                    for e in range(2):
                                        k.mm(V(PS[n2b][:, e * 256:e * 256 + 128], nk_), ntcur[q][e], ncur[q][e],
                                             start=(e == 0))
                                        k.mm(V(PS[n2b][:, e * 256 + 128:(e + 1) * 256], nk_), ncur[q][e], ntcur[q][e],
                                             start=False)
                                    k.cp(V(N2s[g][:], "N2s%d" % g), V(PS[n2b][:, :], nk_))
                                    ncur[q] = [V(N2s[g][:, e * 256:e * 256 + 128], "N2s%d" % g) for e in range(2)]
                                    ntcur[q] = [V(N2s[g][:, e * 256 + 128:(e + 1) * 256], "N2s%d" % g) for e in range(2)]
                            yield
                        ck("pl")
                        for q in range(len(its)):
                            g = gs[q]
                            zb_, zc = 4 + q, 0
                            zk_ = "ps%d" % zb_
                            zv = PS[zb_][:, zc:zc + 256].rearrange("p (e x) -> p e x", e=2)
                            k.act(V(WTs[g][:], "WTs%d" % g), V(zv[:, :, 0:64], zk_), AF.Copy)
                            ck("f1")
                            k.act(V(Ub[g][:], "Ub%d" % g), V(zv[:, :, 64:128], zk_), AF.Copy)
                            ck("f2")
                            k.tr(V(PSB[:, q * 128:(q + 1) * 128], "psb"),
                                 V(Ub[g][:].rearrange("p e x -> p (e x)"), "Ub%d" % g), identb)
                            ck("f3")
                            k.act(V(UT[g][:], "UT%d" % g), V(PSB[:, q * 128:(q + 1) * 128], "psb"), AF.Copy)

                    def step_pair(pi):
                        its = items[GQ * pi:GQ * pi + GQ]
                        for q, (n, qi, is_start, is_end) in enumerate(its):
                            g = (GQ * pi + q) % NSET
                            if is_start:
                                stcount[0] += 1
                            sti = stcount[0] % 2
                            ST = STs[sti]
                            stk = "ST%d" % sti
                            if is_start:
                                if half == 0:
                                    k.memset(V(ST[:], stk), 0.0)
                                else:
                                    k.dma(V(ST[:], stk), V(st0_d[c, d], "st0_d"), key="st%d" % sti)
                            k.act(V(STb[0:64, 0:64], "STb"), V(ST[0:64, :], stk), AF.Copy)
                            k.act(V(STb[64:128, 64:128], "STb"), V(ST[64:128, :], stk), AF.Copy)
                            yield
                            sg_ = stepcount[0] % 2
                            stepcount[0] += 1
                            sat = SAT[sg_]
                            satk = "SAT%d" % sg_
                            B6 = PS[6]
                            k.mm(V(B6[:, 0:128], "ps6"), V(UT[g][:], "UT%d" % g), V(STb[:], "STb"), start=True)
                            k.tt(V(sat[:], satk), V(B6[:, 0:128], "ps6"),
                                 V(WTs[g][:].rearrange("p e x -> p (e x)"), "WTs%d" % g), ALU.add)
                            yield
                            for e in range(2):
                                pr = slice(64 * e, 64 * e + 64)
                                cs = slice(64 * e, 64 * e + 64)
                                k.mm(V(B6[pr, 128:192], "ps6"), V(BHT[:, n, cs], "BHT"), V(sat[:, cs], satk), start=False)
                                k.mm(V(B6[pr, 128:192], "ps6"), V(KHT[:, n, cs], "KHT"), V(VT[:, n, cs], "VT"), start=False)
                            k.mm(V(B6[:, 192:320], "ps6"), V(STb[:], "STb"), V(AR[:, n, 1, :], "AR"), start=False)
                            for e in range(2):
                                pr = slice(64 * e, 64 * e + 64)
                                cs = slice(64 * e, 64 * e + 64)
                                amk = "AM%d_%d" % (g, e)
                                k.mm(V(B6[pr, 192:320], "ps6"), V(sat[:, cs], satk), V(AM[g][e][:, 128:256], amk), start=False)
                                k.mm(V(B6[pr, 192:320], "ps6"), V(VT[:, n, cs], "VT"), V(AM[g][e][:, 384:512], amk), start=False)
                            yield
                            k.stt(V(ST[:], stk), V(ST[:], stk), V(PCt[:, n:n + 1], "PCt"), V(B6[:, 128:192], "ps6"),
                                  ALU.mult, ALU.add)
                            if d == 0:
                                k.act(V(YACC[:, n * 128:(n + 1) * 128], "YACC%d" % (n // 4)), V(B6[:, 192:320], "ps6"), AF.Copy)
                            else:
                                k.tt(V(YACC[:, n * 128:(n + 1) * 128], "YACC%d" % (n // 4)), V(B6[:, 192:320], "ps6"),
                                     V(YACC[:, n * 128:(n + 1) * 128], "YACC%d" % (n // 4)), ALU.add)
                            if is_end and half == 0:
                                k.dma(V(newsT_d[qi, d, c], "news_d"), V(ST[:], stk), key="stout%d" % sti)

                    npairs = len(items) // GQ
                    for _ in pre_pair(0):
                        pass
                    ck("pre0")
                    for pi in range(npairs):
                        gens = []
                        if pi + 1 < npairs:
                            gens.append(pre_pair(pi + 1))
                        gens.append(step_pair(pi))
                        while gens:
                            for g_ in list(gens):
                                try:
                                    next(g_)
                                except StopIteration:
                                    gens.remove(g_)
                ck("scan")
                if c == 0:
                    k.tap("YACC%d" % half, V(YACC[:, 0:NTH], ["YACC0", "YACC1"]))
                def fin_block(j, tE):
                    lo = j * 512
                    yv = V(YACC[:, lo:lo + 512], "YACC%d" % j)
                    pa, pb = 4 + j, 2 + j
                    k.mm(V(PS[pa][:, :], "ps%d" % pa), bones, yv, start=True)
                    yield
                    k.stt(yv, V(PS[pa][:, :], "ps%d" % pa), -1.0 / 64, yv, ALU.mult, ALU.add)
                    yield
                    k.tt(vt(tE), yv, yv, ALU.mult)
                    yield
                    k.mm(V(PS[pb][:, :], "ps%d" % pb), bones, vt(tE), start=True)
                    yield
                    k.act(vt(tE), V(PS[pb][:, :], "ps%d" % pb), AF.Ln, bias=ppc(PP_GNEPS), scale=1.0 / 64)
                    yield
                    k.act(vt(tE), vt(tE), AF.Exp, scale=-0.5)
                    yield
                    k.tt(yv, yv, vt(tE), ALU.mult)
                    yield
                    k.ts(yv, yv, ppc(PP_LNW + c), ppc(PP_LNB + c), ALU.mult, ALU.add)
                    yield
                    k.tt(yv, yv, V(BON[:, lo:lo + 512], "BON"), ALU.add)
                gens = [fin_block(0, "E1"), fin_block(1, "E3")]
                while gens:
                    for g_ in list(gens):
                        try:
                            next(g_)
                        except StopIteration:
                            gens.remove(g_)

                def ev_g(tb, ps):
                    lo = (tb - 2 * half) * 512
                    k.act(vt("E2"), ps, AF.Silu)
                    k.tt(V(UaT[:, c, tb * 512:(tb + 1) * 512], "UaT%d" % tb), V(YACC[:, lo:lo + 512], "YACC%d" % (lo // 512)), vt("E2"),
                         ALU.mult)
                proj(sl_g, tbs, ev_g)
                ck("fin")
        k.tap("UaT", V(UaT[:], ["UaT%d" % i for i in range(4)]), BF16)
        k.s.barrier()
        es2.close()

    if "rwkv" in phases:
        try:
            rwkv_phase()
        except StopBuild:
            k.s.barrier()


    UbT = k.sb("UbT", [128, 8, NT], BF16) if ("attn" in phases or "merge" in phases) else None
    QOFF, KOFF, VOFF, GBOFF, MAOFF, MBOFF = 4224, 5248, 5504, 5760, 6784, 7808

    MNW = 8
    m_wsl = [k.sb("m_wsl%d" % i, [128, 8, 128], BF16) for i in range(MNW)] if "merge" in phases else None
    WOUT = k.sb("WOUT", [128, 8, D], BF16) if "merge" in phases else None
    m_wcount = [0]
    woa_v = woa_d.rearrange("(kc p) n -> p kc n", p=128)
    wob_v = wob_d.rearrange("(kc p) n -> p kc n", p=128)
    wout_v = wout_d.rearrange("(kc p) n -> p kc n", p=128)

    def m_load_w(src_v, col0):
        sl = m_wcount[0] % MNW
        m_wcount[0] += 1
        k.dma(V(m_wsl[sl][:, :, :], "m_wsl%d" % sl), V(src_v[:, :, col0:col0 + 128], "w_d"), key="m_wsl%d" % sl, eng="pool")
        return sl

    def m_loads(dc):
        return (m_load_w(win_v, MAOFF + dc * 128), m_load_w(win_v, MBOFF + dc * 128),
                m_load_w(woa_v, dc * 128), m_load_w(wob_v, dc * 128))
    m_pref = {}

    def merge_prefetch():
        if "merge" not in phases:
            return
        for dc in range(2):
            m_pref[dc] = m_loads(dc)
        for hb in range(2):
            k.dma(V(WOUT[:, :, hb * 512:(hb + 1) * 512], "WOUT"), V(wout_v[:, :, hb * 512:(hb + 1) * 512], "wout_d"),
                  key="WOUT", eng="pool")

    def attn_phase():
        es3 = ExitStack()

        def sbl(name, shape, dt):
            return es3.enter_context(nc.sbuf_tensor("s3_" + name, list(shape), dt))
        NW = 4
        wsl = [sbl("wsl%d" % i, [128, 8, 128], BF16) for i in range(NW)]
        wcount = [0]

        def load_w(col0, dup64=False):
            sl = wcount[0] % NW
            wcount[0] += 1
            if dup64:
                for h_ in range(2):
                    k.dma(V(wsl[sl][:, :, 64 * h_:64 * h_ + 64], "a_wsl%d" % sl), V(win_v[:, :, col0:col0 + 64], "win_d"),
                          key="a_wsl%d" % sl, eng="pool")
            else:
                k.dma(V(wsl[sl][:, :, :], "a_wsl%d" % sl), V(win_v[:, :, col0:col0 + 128], "win_d"),
                      key="a_wsl%d" % sl, eng="pool")
            return sl
        COS = sbl("COS", [128, 1024], F32)
        SIN = sbl("SIN", [128, 1024], F32)
        k.dma(V(COS[:], "COS"), V(rope_d[0], "rope_d"), key="c7")
        k.dma(V(SIN[:], "SIN"), V(rope_d[1], "rope_d"), key="c8")
        perm = V(cstb[:, C_PERM:C_PERM + 128], "cstb")
        esink = sbl("esink", [128, 16], F32)
        k.act(V(esink[:], "esink"), V(pp[:, PP_SINK:PP_SINK + 16], "pp"), AF.Exp)
        VTK = sbl("VTK", [128, 16, 4, 66], BF16)
        KCT = sbl("KCT", [128, 4, 256], BF16)
        CV = sbl("CV", [128, 2, 4, 66], BF16)
        es3b = ExitStack()

        def sblb(name, shape, dt):
            return es3b.enter_context(nc.sbuf_tensor("s3_" + name, list(shape), dt))
        WKV = sblb("WKV", [128, 8, 512], BF16)
        k.dma(V(WKV[:], "WKV"), V(win_v[:, :, KOFF:KOFF + 512], "win_d"), key="WKV", eng="pool")
        k.memset(V(VTK[:, :, :, 64:65], "VTKones"), 1.0)
        k.memset(V(CV[:, :, :, 64:65], "CVones"), 1.0)
        for blk in range(2):
            k.dma(V(CV[:, blk, :, 0:64], "CV"),
                  V(cachev_d[:, blk * 128:(blk + 1) * 128, :].rearrange("h t d -> t h d"), "cv_d"),
                  key="c9", eng="pool")
        CK2 = sblb("CK2", [128, 2, 4, 2, 64], BF16)
        for dup in range(2):
            for blk in range(2):
                k.dma(V(CK2[:, blk, :, dup, :], "CK2"),
                      V(cachek_d[:, blk * 128:(blk + 1) * 128, :].rearrange("h t d -> t h d"), "ck_d"),
                      key="c10", eng="pool")
        for hk in range(4):
            for blk in range(2):
                k.tr(V(PSB[:, (hk * 2 + blk) * 128:(hk * 2 + blk + 1) * 128], "psb"),
                     V(CK2[:, blk, hk, :, :].rearrange("p a d -> p (a d)"), "CK2"), identb)
        k.act(V(KCT[:].rearrange("p h t -> p (h t)"), "KCT"), V(PSB[:, :], "psb"), AF.Copy)
        stg = [sblb("kvstg%d" % i, [128, 512], F32) for i in range(2)]
        for t in range(16):
            b_ = t % 2
            bk = "ps%d" % b_
            for kc in range(8):
                k.mm(V(PS[b_][:, :], bk), V(hT[:, kc, t * 128:(t + 1) * 128], "hT%d" % t), V(WKV[:, kc, :], "WKV"),
                     start=(kc == 0), stop=(kc == 7))
            k.act(V(stg[b_][:], "kvstg%d" % b_), V(PS[b_][:, :], bk), AF.Copy)
            k.cp(V(VTK[:, t, :, 0:64], "VTK%d" % t), V(stg[b_][:, 256:512].rearrange("p (h d) -> p h d", d=64), "kvstg%d" % b_))
            if t < 8:
                sq, hf = t // 2, t % 2
                k.dma(V(newk_d[sq, :, hf * 128:(hf + 1) * 128, :].rearrange("h t d -> t h d"), "newk_d"),
                      V(stg[b_][:, 0:256].rearrange("p (h d) -> p h d", d=64), "kvstg%d" % b_), key="kvout%d" % b_)
                k.dma(V(newv_d[sq, :, hf * 128:(hf + 1) * 128, :].rearrange("h t d -> t h d"), "newv_d"),
                      V(stg[b_][:, 256:512].rearrange("p (h d) -> p h d", d=64), "kvstg%d" % b_), key="kvout%d" % b_)
        ck("kv")
        k.s.barrier()
        es3b.close()
        QT = [sbl("QT%d" % i, [128, NT], BF16) for i in range(2)]
        KT2 = sbl("KT2", [128, NT], BF16)
        WG = sbl("WG", [128, 8, 256], BF16)
        QB16 = sbl("QB16", [128, 512], BF16)
        T1 = sbl("T1", [128, 512], F32)
        T2 = sbl("T2", [128, 512], F32)
        Et = [sbl("E%d" % i, [128, 512], BF16) for i in range(3)]
        DEN = sbl("DEN", [128, 4], F32)
        YBt = sbl("YBt", [128, 256], F32)
        SGall = sbl("SGall", [128, 16, 256], BF16)
        UBt = sbl("UBt", [128, 256], BF16)
        ecount = [0]
        pending = [None]
        for hk in range(4):
            slq = [load_w(QOFF + (2 * hk + i) * 128) for i in range(2)]
            slk = load_w(KOFF + hk * 64, dup64=True)
            k.dma(V(WG[:], "WG"), V(win_v[:, :, GBOFF + hk * 256:GBOFF + (hk + 1) * 256], "win_d"), key="WG", eng="pool")
            ck("aload")
            for (sl, dst, dk) in ((slq[0], QT[0], "QT0"), (slq[1], QT[1], "QT1"), (slk, KT2, "KT2")):
                for tb in range(4):
                    if tb == 2:
                        ck("aq01")
                    if tb == 3:
                        ck("aq2")
                    bank = tb % 2
                    bk = "ps%d" % bank
                    for kc in range(8):
                        k.mm(V(PS[bank][:, :], bk), V(wsl[sl][:, kc, :], "a_wsl%d" % sl),
                             V(hT[:, kc, tb * 512:(tb + 1) * 512], hTk(tb)), start=(kc == 0), stop=(kc == 7))
                    ps = V(PS[bank][:, :], bk)
                    dv = V(dst[:, tb * 512:(tb + 1) * 512], dk)
                    if tb < 2:
                        k.act(dv, ps, AF.Copy)
                    else:
                        s0 = (tb - 2) * 512
                        k.act(V(QB16[:], "QB16"), ps, AF.Copy)
                        k.act(V(T1[:], "T1"), ps, AF.Copy)
                        k.tt(V(T1[:], "T1"), V(T1[:], "T1"), V(COS[:, s0:s0 + 512], "COS"), ALU.mult)
                        k.mm(V(PS[6][:, :], "ps6"), perm, V(QB16[:], "QB16"), start=True)
                        k.act(V(T2[:], "T2"), V(PS[6][:, :], "ps6"), AF.Copy)
                        k.tt(V(T2[:], "T2"), V(T2[:], "T2"), V(SIN[:, s0:s0 + 512], "SIN"), ALU.mult)
                        k.tt(dv, V(T1[:], "T1"), V(T2[:], "T2"), ALU.add)
            ck("aproj")
            for t in range(16):
                gbk = 6 if t % 2 == 0 else 3
                for kc in range(8):
                    k.mm(V(PS[gbk][:, 0:256], "ps%d" % gbk), V(hT[:, kc, t * 128:(t + 1) * 128], "hT%d" % t),
                         V(WG[:, kc, :], "WG"), start=(kc == 0), stop=(kc == 7))
                k.act(V(SGall[:, t, :], "SGall"), V(PS[gbk][:, 0:256], "ps%d" % gbk), AF.Silu)
            if hk == 0:
                k.tap("QT0", V(QT[0][:], "QT0"), BF16)
                k.tap("KT2", V(KT2[:], "KT2"), BF16)
            for t in range(16):
                kbs = []
                if t < 8:
                    sq = t // 2
                    for kt in (2 * sq, 2 * sq + 1):
                        kbs.append(("loc", kt, None))
                else:
                    i = t - 8
                    if i - 1 >= 0:
                        kbs.append(("loc", t - 1, C_TRI_GE))
                    kbs.append(("loc", t, None))
                    if i + 1 < 8:
                        kbs.append(("loc", t + 1, C_TRI_LE))
                    kbs.append(("ctx", 0, None))
                    kbs.append(("ctx", 1, None))
                ob = 4 + (t % 2)
                obk = "ps%d" % ob
                def scores(ki):
                    kind, kt, msk = kbs[ki]
                    sb_ = 2 * (ki % 2)
                    E = Et[ecount[0] % 3]
                    ek = "E%d" % (ecount[0] % 3)
                    ecount[0] += 1
                    Ev = E[:].rearrange("p (c e q) -> p c e q", c=2, e=2)
                    for e in range(2):
                        pr = slice(64 * e, 64 * e + 64)
                        bk = "ps%d" % (sb_ + e)
                        for cl in range(2):
                            if kind == "loc":
                                lhs = V(KT2[pr, kt * 128:(kt + 1) * 128], "KT2")
                            else:
                                lhs = V(KCT[pr, hk, kt * 128:(kt + 1) * 128], "KCT")
                            k.mm(V(PS[sb_ + e][:, cl * 128:(cl + 1) * 128], bk), lhs,
                                 V(QT[cl][pr, t * 128:(t + 1) * 128], "QT%d" % cl), start=(cl == 0))
                        k.act(V(Ev[:, :, e, :], ek), V(PS[sb_ + e][:, 0:256].rearrange("p (c q) -> p c q", c=2), bk),
                              AF.Exp, scale=0.125)
                    if msk is not None:
                        mv = V(cstb[:, msk:msk + 128].unsqueeze(1).to_broadcast([128, 4, 128]), "cstb")
                        k.tt(V(E[:].rearrange("p (h q) -> p h q", h=4), ek), V(E[:].rearrange("p (h q) -> p h q", h=4), ek),
                             mv, ALU.mult)
                    return Ev, ek

                def pv(ki, Ev, ek):
                    kind, kt, msk = kbs[ki]
                    for cl in range(2):
                        for e in range(2):
                            h4 = cl * 2 + e
                            if kind == "loc":
                                rhs = V(VTK[:, kt, hk, 0:65], ["VTK%d" % kt, "VTKones"])
                            else:
                                rhs = V(CV[:, kt, hk, 0:65], ["CV", "CVones"])
                            k.mm(V(PS[ob][:, h4 * 66:h4 * 66 + 65], obk), V(Ev[:, cl, e, :], ek), rhs,
                                 start=(ki == 0 and h4 == 0))
                cur = scores(0)
                for ki in range(len(kbs)):
                    nxt = scores(ki + 1) if ki + 1 < len(kbs) else None
                    pv(ki, *cur)
                    cur = nxt
                def finalize(t=t, ob=ob, obk=obk, hk=hk):
                    ov = PS[ob][:, 0:264].rearrange("p (h x) -> p h x", x=66)
                    k.tt(V(DEN[:], "DEN"), V(ov[:, :, 64], obk), V(esink[:, 4 * hk:4 * hk + 4], "esink"), ALU.add)
                    k.recip(V(DEN[:], "DEN"), V(DEN[:], "DEN"))
                    k.tt(V(YBt[:].rearrange("p (h d) -> p h d", d=64), "YBt"), V(ov[:, :, 0:64], obk),
                         V(DEN[:].unsqueeze(2).to_broadcast([128, 4, 64]), "DEN"), ALU.mult)
                    k.tt(V(UBt[:], "UBt"), V(YBt[:], "YBt"), V(SGall[:, t, :], "SGall"), ALU.mult)
                    for i in range(2):
                        k.tr(V(PSB[:, i * 128:(i + 1) * 128], "psb"), V(UBt[:, i * 128:(i + 1) * 128], "UBt"), identb)
                    k.act(V(UbT[:, 2 * hk:2 * hk + 2, t * 128:(t + 1) * 128], "UbT%d" % (t // 4)),
                          V(PSB[:, 0:256].rearrange("p (i q) -> p i q", i=2), "psb"), AF.Copy)

                if pending[0] is not None:
                    pending[0]()
                pending[0] = finalize
            if pending[0] is not None:
                pending[0]()
                pending[0] = None
        k.tap("UbT", V(UbT[:], ["UbT%d" % i for i in range(4)]), BF16)
        merge_prefetch()
        k.s.barrier()
        es3.close()

    if "attn" in phases:
        try:
            attn_phase()
        except StopBuild:
            k.s.barrier()

    def merge_phase():
        es4 = ExitStack()

        def sbl(name, shape, dt):
            return es4.enter_context(nc.sbuf_tensor("s4_" + name, list(shape), dt))
        wsl = m_wsl
        if not m_pref:
            merge_prefetch()
        FNW = sbl("FNW", [128, D], F32)
        k.dma(V(FNW[:], "FNW"), V(fnw_d, "fnw_d"), key="c11")
        mergedT = sbl("mergedT", [128, 8, NT], BF16)
        SMA = sbl("SMA", [128, 512], F32)
        SMB = sbl("SMB", [128, 512], F32)
        TA = sbl("TA", [128, 512], F32)
        TB = sbl("TB", [128, 512], F32)
        xt = [sbl("xt%d" % i, [128, D], F32) for i in range(2)]
        xs = [sbl("xs%d" % i, [128, D], F32) for i in range(2)]
        junk = sbl("junk", [128, D], F32)
        ssq = sbl("ssq", [128, 16], F32)
        for half in range(1):
            for dc in range(8):
                if dc + 1 < 8 and (dc + 1) not in m_pref:
                    m_pref[dc + 1] = m_loads(dc + 1)
                s_ma, s_mb, s_oa, s_ob = m_pref[dc]
                for j in range(4):
                    tb = j
                    tsl = slice(tb * 512, (tb + 1) * 512)
                    for (sl, src, srck, bank) in ((s_ma, hT, hTk(tb), 0), (s_mb, hT, hTk(tb), 1),
                                                  (s_oa, UaT, ["UaT%d" % tb], 2), (s_ob, UbT, ["UbT%d" % tb], 3)):
                        for kc in range(8):
                            k.mm(V(PS[bank][:, :], "ps%d" % bank), V(wsl[sl][:, kc, :], "m_wsl%d" % sl),
                                 V(src[:, kc, tsl], srck), start=(kc == 0), stop=(kc == 7))
                    k.act(V(SMA[:], "SMA"), V(PS[0][:, :], "ps0"), AF.Sigmoid)
                    k.act(V(SMB[:], "SMB"), V(PS[1][:, :], "ps1"), AF.Sigmoid)
                    k.tt(V(TA[:], "TA"), V(PS[2][:, :], "ps2"), V(SMA[:], "SMA"), ALU.mult)
                    k.tt(V(TB[:], "TB"), V(PS[3][:, :], "ps3"), V(SMB[:], "SMB"), ALU.mult)
                    k.tt(V(mergedT[:, dc, j * 512:(j + 1) * 512], "mergedT"), V(TA[:], "TA"), V(TB[:], "TB"), ALU.add)
            if half == 0:
                k.tap("mergedT", V(mergedT[:], "mergedT"), BF16)
            for tl in range(16):
                t = tl
                r = 0 if t < 8 else 1
                xb = t % 2
                k.dma(V(xt[xb][:], "xt%d" % xb), V(x_d[t * 128:(t + 1) * 128, :], "x_d"), key="xt%d" % xb)
                for nb in range(2):
                    bank = 4 + nb
                    for kc in range(8):
                        k.mm(V(PS[bank][:, :], "ps%d" % bank), V(mergedT[:, kc, tl * 128:(tl + 1) * 128], "mergedT"),
                             V(WOUT[:, kc, nb * 512:(nb + 1) * 512], "WOUT"), start=(kc == 0), stop=(kc == 7))
                    xv = V(xs[xb][:, nb * 512:(nb + 1) * 512], "xs%d" % xb)
                    k.tt(xv, V(PS[bank][:, :], "ps%d" % bank), V(gate_bc[:, r, nb * 512:(nb + 1) * 512], "gate_bc"), ALU.mult)
                    k.tt(xv, xv, V(xt[xb][:, nb * 512:(nb + 1) * 512], "xt%d" % xb), ALU.add)
                k.tt(V(junk[:], "junk4"), V(xs[xb][:], "xs%d" % xb), V(xs[xb][:], "xs%d" % xb), ALU.mult)
                k.rsum(V(ssq[:, t:t + 1], "ssq%d" % t), V(junk[:], "junk4"))
                k.act(V(ssq[:, t:t + 1], "ssq%d" % t), V(ssq[:, t:t + 1], "ssq%d" % t), AF.Sqrt, bias=RMS_EPS, scale=1.0 / D)
                k.recip(V(ssq[:, t:t + 1], "ssq%d" % t), V(ssq[:, t:t + 1], "ssq%d" % t))
                k.act(V(xs[xb][:], "xs%d" % xb), V(xs[xb][:], "xs%d" % xb), AF.Copy, scale=V(ssq[:, t:t + 1], "ssq%d" % t))
                k.tt(V(xs[xb][:], "xs%d" % xb), V(xs[xb][:], "xs%d" % xb), V(FNW[:], "FNW"), ALU.mult)
                k.dma(V(y_d[t * 128:(t + 1) * 128, :], "y_d"), V(xs[xb][:], "xs%d" % xb), key="yout%d" % xb)
        k.s.barrier()
        es4.close()

    if "merge" in phases:
        try:
            merge_phase()
        except StopBuild:
            k.s.barrier()

    k.emit()
    return nc, es, k


def host_layout(inputs, core):
    f = lambda a: np.ascontiguousarray(np.asarray(a, dtype=np.float32))
    b = core % 4
    m = {}
    xp = f(inputs["x_prompt"])[4 * core:4 * core + 4].reshape(1024, D)
    xs = f(inputs["x_sample"])[b]
    m["x"] = np.ascontiguousarray(np.concatenate([xp, xs], 0))
    cond = np.stack([f(inputs["c_ctx"]), f(inputs["c"])[b]], 0)
    m["condT"] = np.ascontiguousarray(cond.reshape(2, 8, 128).transpose(2, 1, 0).reshape(128, 16))
    m["w_ada"] = f(inputs["w_ada"])[0]
    m["bada2"] = np.ascontiguousarray(np.tile(f(inputs["b_ada"])[0][None], (2, 1)))
    m["w_in"] = f(inputs["w_in"])[0]
    pp = np.zeros((128, PP_N), np.float32)

    def fm(v, n):
        return v.reshape(n, 128).T
    pp[:, PP_NORMW:PP_NORMW + 8] = fm(f(inputs["norm_w"])[0], 8)
    ca = f(inputs["conv_a"])[0]
    pp[:, PP_CONV:PP_CONV + 75] = ca.reshape(3, 25, 128).transpose(2, 1, 0).reshape(128, 75)
    pp[:, PP_KK:PP_KK + 8] = fm(f(inputs["k_k"])[0], 8)
    pp[:, PP_KA:PP_KA + 8] = fm(f(inputs["k_a"])[0], 8)
    pp[:, PP_RK:PP_RK + 8] = fm(f(inputs["r_k"])[0].reshape(-1), 8)
    pp[:, PP_LNW:PP_LNW + 8] = fm(f(inputs["ln_x_w"])[0], 8)
    pp[:, PP_LNB:PP_LNB + 8] = fm(f(inputs["ln_x_b"])[0], 8)
    pp[:, PP_W0:PP_W0 + 16] = f(inputs["w0"])[0].reshape(2, 8, 128).transpose(2, 0, 1).reshape(128, 16)
    pp[:, PP_A0:PP_A0 + 16] = f(inputs["a0"])[0].reshape(2, 8, 128).transpose(2, 0, 1).reshape(128, 16)
    pp[:, PP_SINK:PP_SINK + 16] = f(inputs["sink"])[0].reshape(1, 16)
    m["pp"] = pp
    lora = np.concatenate([f(inputs["w_up"])[0], f(inputs["a_up"])[0]], 1)
    m["lora"] = np.ascontiguousarray(lora.transpose(1, 0, 2))
    st = f(inputs["state_rwkv"])[b, 0]
    m["st0"] = np.ascontiguousarray(st.reshape(2, 8, 2, 64, 64).transpose(1, 0, 2, 4, 3).reshape(8, 2, 128, 64))
    m["cst"] = make_consts()
    m["rope"] = make_rope()
    m["cache_k"] = np.ascontiguousarray(f(inputs["cache_k"])[b, 0])
    m["cache_v"] = np.ascontiguousarray(f(inputs["cache_v"])[b, 0])
    m["w_oA"] = f(inputs["w_oA"])[0]
    m["w_oB"] = f(inputs["w_oB"])[0]
    m["w_out"] = f(inputs["w_out"])[0]
    m["fnw"] = np.ascontiguousarray(np.tile(f(inputs["final_norm_w"])[None], (128, 1)))
    return m


_ROPE = None


def _partner(p):
    j = (p % 64) % 32
    return p + 16 if j < 16 else p - 16


def make_rope():
    global _ROPE
    if _ROPE is not None:
        return _ROPE
    t = np.arange(1024)
    row = (t // 64).astype(np.float32)
    col = (t % 64).astype(np.float32)
    inv = (np.float32(10000.0) ** (-np.arange(16, dtype=np.float32) / np.float32(16))).astype(np.float32)
    r = np.zeros((2, 128, 1024), np.float32)
    for p in range(128):
        d = p % 64
        half, j = d // 32, d % 32
        fq, part = j % 16, j // 16
        pos = row if half == 0 else col
        ang = (pos * inv[fq]).astype(np.float32)
        r[0, p] = np.cos(ang)
        r[1, p] = np.sin(ang) * (-1.0 if part == 0 else 1.0)
    _ROPE = r
    return r


_CST = None


def make_consts():
    global _CST
    if _CST is not None:
        return _CST
    c = np.zeros((128, C_N), np.float32)
    c[:, C_IDENT:C_IDENT + 128] = np.eye(128)
    bo = np.zeros((128, 128), np.float32)
    bo[:64, :64] = 1
    bo[64:, 64:] = 1
    c[:, C_BONES:C_BONES + 128] = bo
    p = np.arange(128)[:, None]
    q = np.arange(128)[None, :]
    fs, fi = (p < q), (p <= q)
    bs, bi = (p > q), (p >= q)
    c[:, C_MX0:C_MX0 + 512] = np.concatenate([fs, fi, fs, fi], 1)
    c[:, C_MX1:C_MX1 + 512] = np.concatenate([bs, bi, bs, bi], 1)
    c[:, C_MT0:C_MT0 + 256] = np.concatenate([q < p, q < p], 1)
    c[:, C_MT1:C_MT1 + 256] = np.concatenate([q > p, q > p], 1)
    c[:, C_TRI_GE:C_TRI_GE + 128] = (p >= q)
    c[:, C_TRI_LE:C_TRI_LE + 128] = (p <= q)
    for m_ in range(128):
        c[_partner(m_), C_PERM + m_] = 1.0
    _CST = c
    return c


def assemble(results):
    y_prompt = np.zeros((32, 256, D), np.float32)
    y_sample = np.zeros((4, 1024, D), np.float32)
    new_k = np.zeros((32, 1, 4, 256, 64), np.float32)
    new_v = np.zeros((32, 1, 4, 256, 64), np.float32)
    new_s = np.zeros((32, 1, 2, 16, 64, 64), np.float32)
    for i in range(8):
        r = results[i]
        y = np.asarray(r["y"], np.float32)
        y_prompt[4 * i:4 * i + 4] = y[:1024].reshape(4, 256, D)
        if i < 4:
            y_sample[i] = y[1024:]
        new_k[4 * i:4 * i + 4, 0] = np.asarray(r["newk"], np.float32)
        new_v[4 * i:4 * i + 4, 0] = np.asarray(r["newv"], np.float32)
        st = np.asarray(r["newsT"], np.float32).reshape(4, 2, 8, 2, 64, 64)
        new_s[4 * i:4 * i + 4, 0] = st.transpose(0, 1, 2, 3, 5, 4).reshape(4, 2, 16, 64, 64)
    return (y_prompt, y_sample, new_k, new_v, new_s)


def kernel(**inputs):
    nc, es, k = build()
    in_maps = [host_layout(inputs, i) for i in range(8)]
    res = run_bass_kernel_spmd(nc, in_maps, core_ids=list(range(8)))
    return assemble(res.results)
```

```python
import numpy as np
from contextlib import ExitStack
import concourse.bass as bass
import concourse.mybir as mybir
from concourse.bass_utils import run_bass_kernel_spmd

F32 = mybir.dt.float32
BF16 = mybir.dt.bfloat16
AF = mybir.ActivationFunctionType
ALU = mybir.AluOpType
AX = mybir.AxisListType

NT = 2048
D = 1024
KAPPA = float(np.exp(-0.5))
RMS_EPS = 1e-6
GN_EPS = 64e-5
INW = 8832

PP_NORMW = 0
PP_CONV = 8
PP_KK = PP_CONV + 75
PP_KA = PP_KK + 8
PP_RK = PP_KA + 8
PP_LNW = PP_RK + 8
PP_LNB = PP_LNW + 8
PP_W0 = PP_LNB + 8
PP_A0 = PP_W0 + 16
PP_SINK = PP_A0 + 16
PP_N = PP_SINK + 16

C_IDENT = 0
C_BONES = 128
C_MX0 = 256
C_MX1 = C_MX0 + 512
C_MT0 = C_MX1 + 512
C_MT1 = C_MT0 + 256
C_TRI_GE = C_MT1 + 256
C_TRI_LE = C_TRI_GE + 128
C_PERM = C_TRI_LE + 128
C_N = C_PERM + 128


class V:
    __slots__ = ("ap", "keys")

    def __init__(self, ap, keys):
        self.ap = ap
        self.keys = list(keys) if isinstance(keys, (list, tuple)) else [keys]


class Instr:
    __slots__ = ("eng", "fn", "waits", "signal", "tick", "dma_key", "dma_tick", "seq")


class Sched:
    ENG = ["pe", "act", "dve", "pool", "sp"]

    def __init__(self):
        self.streams = {e: [] for e in self.ENG}
        self.last_w = {}
        self.readers = {}
        self.dma_counts = {}
        self.last_dma = {}
        self.nseq = 0

    def barrier(self):
        lasts = []
        for e in self.ENG:
            for ins in reversed(self.streams[e]):
                if ins.dma_key is None and ins.fn is not None:
                    lasts.append(ins)
                    break
        lasts.extend(self.last_dma.values())
        for e in self.ENG:
            b = Instr()
            b.eng, b.fn, b.waits, b.signal, b.tick, b.dma_key, b.dma_tick = e, None, [], False, 0, None, 0
            self.nseq += 1
            b.seq = self.nseq
            for d in lasts:
                if d.dma_key is None and d.eng == e:
                    continue
                b.waits.append(d)
                d.signal = True
            self.streams[e].append(b)

    def add(self, eng, fn, reads=(), writes=(), dma_key=None):
        ins = Instr()
        ins.eng = eng
        ins.fn = fn
        ins.waits = []
        ins.signal = False
        ins.tick = 0
        ins.dma_key = dma_key
        ins.dma_tick = 0
        self.nseq += 1
        ins.seq = self.nseq
        if dma_key is not None:
            self.dma_counts[dma_key] = self.dma_counts.get(dma_key, 0) + 1
            ins.dma_tick = 16 * self.dma_counts[dma_key]
            self.last_dma[dma_key] = ins
        deps = []
        for r in reads:
            w = self.last_w.get(r)
            if w is not None:
                deps.append((w, "raw"))
            if isinstance(r, str) and r.startswith("ps") and eng in ("act", "dve"):
                for q in self.readers.get(r, ()):
                    if q.eng != eng and q.eng in ("act", "dve"):
                        deps.append((q, "rar"))
        for r in writes:
            w = self.last_w.get(r)
            if w is not None:
                deps.append((w, "waw"))
            for q in self.readers.get(r, ()):
                deps.append((q, "war"))
        best = {}
        for d, kind in deps:
            if d is ins:
                continue
            if d.dma_key is None and d.eng == eng and dma_key is None:
                if eng == "pe" or kind != "raw":
                    continue
            kk_ = ("dma", d.dma_key) if d.dma_key is not None else d.eng
            cur = best.get(kk_)
            if cur is None or d.seq > cur.seq:
                best[kk_] = d
        for d in best.values():
            ins.waits.append(d)
            d.signal = True
        for r in reads:
            self.readers.setdefault(r, []).append(ins)
        for r in writes:
            self.last_w[r] = ins
            self.readers[r] = []
        self.streams[eng].append(ins)
        return ins


class KB:
    def __init__(self, nc, es, taps):
        self.nc = nc
        self.es = es
        self.s = Sched()
        self.taps = taps
        self.tap_out = {}
        self.ndma = 0

    def sb(self, name, shape, dt):
        return self.es.enter_context(self.nc.sbuf_tensor("s_s_" + name, list(shape), dt))

    def pt(self, name, shape, dt):
        return self.es.enter_context(self.nc.psum_tensor("p_" + name, list(shape), dt))

    @staticmethod
    def _rk(*ops):
        ks = []
        for o in ops:
            if isinstance(o, V):
                ks.extend(o.keys)
        return ks

    @staticmethod
    def _a(o):
        return o.ap if isinstance(o, V) else o

    def mm(self, out, lhsT, rhs, start, stop=True):
        o, l, r = out.ap, lhsT.ap, rhs.ap
        self.s.add("pe", lambda e: e.matmul(o, l, r, start=start, stop=stop),
                   self._rk(lhsT, rhs), out.keys)

    def tr(self, out, in_, ident):
        o, i, d = out.ap, in_.ap, ident.ap
        self.s.add("pe", lambda e: e.transpose(o, i, d), self._rk(in_, ident), out.keys)

    def act(self, out, in_, func, bias=0.0, scale=1.0, accum=None, eng="act"):
        o, i = out.ap, in_.ap
        b, sc = self._a(bias), self._a(scale)
        ac = self._a(accum) if accum is not None else None
        w = out.keys + (accum.keys if accum is not None else [])

        def f(e):
            if ac is not None:
                return e.activation(o, i, func, bias=b, scale=sc, accum_out=ac)
            return e.activation(o, i, func, bias=b, scale=sc)
        self.s.add(eng, f, self._rk(in_, bias, scale), w)

    def tt(self, out, in0, in1, op, eng="dve"):
        o, a, b = out.ap, in0.ap, in1.ap
        self.s.add(eng, lambda e: e.tensor_tensor(o, a, b, op), self._rk(in0, in1), out.keys)

    def ts(self, out, in0, s1, s2, op0, op1=None, accum=None, eng="dve"):
        o, a = out.ap, in0.ap
        x1, x2 = self._a(s1), self._a(s2)
        ac = self._a(accum) if accum is not None else None
        w = out.keys + (accum.keys if accum is not None else [])

        def f(e):
            kw = {}
            if op1 is not None:
                kw["op1"] = op1
            if ac is not None:
                kw["accum_out"] = ac
            return e.tensor_scalar(o, a, x1, x2, op0, **kw)
        self.s.add(eng, f, self._rk(in0, s1, s2), w)

    def stt(self, out, in0, scalar, in1, op0, op1, accum=None):
        o, a, b = out.ap, in0.ap, in1.ap
        sc = self._a(scalar)
        ac = self._a(accum) if accum is not None else None
        w = out.keys + (accum.keys if accum is not None else [])

        def f(e):
            if ac is not None:
                return e.scalar_tensor_tensor(o, a, sc, b, op0, op1, accum_out=ac)
            return e.scalar_tensor_tensor(o, a, sc, b, op0, op1)
        self.s.add("dve", f, self._rk(in0, scalar, in1), w)

    def cp(self, out, in_, eng="dve"):
        o, i = out.ap, in_.ap
        if eng == "act":
            self.s.add("act", lambda e: e.copy(o, i), in_.keys, out.keys)
        else:
            self.s.add(eng, lambda e: e.tensor_scalar(o, i, 1.0, None, ALU.mult), in_.keys, out.keys)

    def rsum(self, out, in_):
        o, i = out.ap, in_.ap
        self.s.add("dve", lambda e: e.reduce_sum(o, i, AX.X), in_.keys, out.keys)

    def recip(self, out, in_):
        o, i = out.ap, in_.ap
        self.s.add("dve", lambda e: e.reciprocal(o, i), in_.keys, out.keys)

    def scan(self, out, d0, d1, init, op0, op1):
        o, a, b = out.ap, d0.ap, d1.ap
        self.s.add("dve", lambda e: e.tensor_tensor_scan(o, a, b, init, op0, op1),
                   self._rk(d0, d1), out.keys)

    def memset(self, out, val, eng="dve"):
        o = out.ap
        self.s.add(eng, lambda e: e.memset(o, val), [], out.keys)

    def dma(self, out, in_, key=None, eng="sp", **kw):
        o, i = out.ap, in_.ap
        if key is None:
            key = "d%d" % self.ndma
        self.ndma += 1
        if eng == "pool":
            kw.setdefault("max_dma_last_dim", 2048)
        self.s.add(eng, lambda e: e.dma_start(out=o, in_=i, **kw), in_.keys, out.keys, dma_key=key)

    def tap(self, name, v, dt=F32):
        if name not in self.taps:
            return
        shape = list(v.ap.shape)
        t = self.nc.dram_tensor("tap_" + name, shape, dt, kind="ExternalOutput").ap()
        self.tap_out[name] = shape
        self.dma(V(t, "tapdram_" + name), v, key="tap")

    def emit(self):
        nc, s = self.nc, self.s
        es = self.es
        engsem = {e: es.enter_context(nc.semaphore("sem_" + e)) for e in Sched.ENG}
        dmasem = {k: es.enter_context(nc.semaphore("dsem_%d" % i)) for i, k in enumerate(s.dma_counts)}
        for e in Sched.ENG:
            c = 0
            for ins in s.streams[e]:
                if ins.dma_key is None and ins.signal and ins.fn is not None:
                    c += 1
                    ins.tick = c
        block = es.enter_context(nc.Block())
        engobj = {"pe": block.tensor, "act": block.scalar, "dve": block.vector, "pool": block.gpsimd,
                  "sp": block.sync}

        def mk(ename):
            def body(e):
                waited = {}
                for ins in s.streams[ename]:
                    for d in ins.waits:
                        if d.dma_key is not None:
                            k, val, sem = ("dma", d.dma_key), d.dma_tick, dmasem[d.dma_key]
                        else:
                            k, val, sem = d.eng, d.tick, engsem[d.eng]
                        if waited.get(k, 0) >= val:
                            continue
                        e.wait_ge(sem, val)
                        waited[k] = val
                    if ins.fn is None:
                        continue
                    bi = ins.fn(e)
                    if ins.dma_key is not None:
                        bi.then_inc(dmasem[ins.dma_key], 16)
                    elif ins.signal:
                        bi.then_inc(engsem[ename], 1)
                if ename == "sp":
                    for k, cnt in s.dma_counts.items():
                        e.wait_ge(dmasem[k], 16 * cnt)
            return body
        for ename in Sched.ENG:
            engobj[ename](mk(ename))


class StopBuild(Exception):
    pass


def build(taps=(), phases=("p0", "p1", "rwkv", "attn", "merge"), stop=None):
    def ck(name):
        if stop == name:
            raise StopBuild()
    nc = bass.Bass("TRN2", target_bir_lowering=False)
    es = ExitStack()
    k = KB(nc, es, set(taps))

    def din(name, shape):
        return nc.dram_tensor(name, list(shape), F32, kind="ExternalInput").ap()

    def dout(name, shape):
        return nc.dram_tensor(name, list(shape), F32, kind="ExternalOutput").ap()

    x_d = din("x", [NT, D])
    condT_d = din("condT", [128, 16])
    wada_d = din("w_ada", [D, 3 * D])
    bada_d = din("bada2", [2, 3 * D])
    win_d = din("w_in", [D, INW])
    pp_d = din("pp", [128, PP_N])
    lora_d = din("lora", [128, 2, 1024])
    st0_d = din("st0", [8, 2, 128, 64])
    cst_d = din("cst", [128, C_N])
    rope_d = din("rope", [2, 128, 1024])
    cachek_d = din("cache_k", [4, 256, 64])
    cachev_d = din("cache_v", [4, 256, 64])
    woa_d = din("w_oA", [D, D])
    wob_d = din("w_oB", [D, D])
    wout_d = din("w_out", [D, D])
    fnw_d = din("fnw", [128, D])
    newk_d = dout("newk", [4, 4, 256, 64])
    newv_d = dout("newv", [4, 4, 256, 64])
    y_d = dout("y", [NT, D])
    newsT_d = dout("newsT", [4, 2, 8, 128, 64])

    win_v = win_d.rearrange("(kc p) n -> p kc n", p=128)

    cst = k.sb("cst", [128, 256], F32)
    pp = k.sb("pp", [128, PP_N + 160], F32)
    PP_NEGW0 = PP_N
    PP_NEGW2 = PP_N + 25
    PP_OMKA = PP_N + 50
    PP_GNEPS = PP_N + 58
    k.dma(V(cst[:], "cst"), V(cst_d[:, 0:256], "cst_d"), key="c1")
    k.dma(V(pp[:, 0:PP_N], "pp"), V(pp_d, "pp_d"), key="c2")
    cstb = k.sb("cstb", [128, C_N], BF16)
    k.dma(V(cstb[:], "cstb"), V(cst_d, "cst_d"), key="c3", eng="pool")
    convv = pp[:, PP_CONV:PP_CONV + 75].rearrange("p (c j) -> p c j", j=3)
    k.ts(V(pp[:, PP_NEGW0:PP_NEGW0 + 25], "pp2"), V(convv[:, :, 0], "pp"), -1.0, None, ALU.mult)
    k.ts(V(pp[:, PP_NEGW2:PP_NEGW2 + 25], "pp2"), V(convv[:, :, 2], "pp"), -1.0, None, ALU.mult)
    k.ts(V(pp[:, PP_OMKA:PP_OMKA + 8], "pp2"), V(pp[:, PP_KA:PP_KA + 8], "pp"), -1.0, 1.0, ALU.mult, ALU.add)
    k.memset(V(pp[:, PP_GNEPS:PP_GNEPS + 1], "pp2"), GN_EPS)
    PPK = ["pp", "pp2"]

    def ppc(col):
        return V(pp[:, col:col + 1], PPK)

    identf = V(cst[:, C_IDENT:C_IDENT + 128], "cst")
    identb = V(cstb[:, C_IDENT:C_IDENT + 128], "cstb")
    bones = V(cst[:, C_BONES:C_BONES + 128], "cst")

    PS = [k.pt("ps%d" % i, [128, 512], F32) for i in range(7)]
    PSB = k.pt("psb", [128, 1024], BF16)

    hT = k.sb("hT", [128, 8, NT], BF16)

    def hTk(tb):
        return ["hT%d" % (4 * tb + i) for i in range(4)]

    mT = k.sb("mT", [128, 48], F32)
    gmod = k.sb("gmod", [128, 8, 2], F32)
    gate_bc = k.sb("gate_bc", [128, 2, D], BF16)
    with ExitStack() as es0:
        sc = es0.enter_context(nc.sbuf_tensor("s_sc", [128, 16], F32))
        wada = [es0.enter_context(nc.sbuf_tensor("s_wada%d" % i, [128, 8, 512], F32)) for i in range(2)]
        m_sb = es0.enter_context(nc.sbuf_tensor("s_m_sb", [2, 3 * D], F32))
        bada = es0.enter_context(nc.sbuf_tensor("s_bada", [2, 3 * D], F32))
        sel = es0.enter_context(nc.sbuf_tensor("s_sel", [2, 2, 128], F32))
        k.dma(V(sc[:], "sc"), V(condT_d, "condT_d"), key="c4")
        k.dma(V(bada[:], "bada"), V(bada_d, "bada_d"), key="c5")
        k.act(V(sc[:], "sc"), V(sc[:], "sc"), AF.Silu)
        scv = sc[:].rearrange("p (c r) -> p c r", r=2)
        wada_v = wada_d.rearrange("(kc p) n -> p kc n", p=128)
        for blk in range(6):
            sl = blk % 2
            k.dma(V(wada[sl][:], "wada%d" % sl), V(wada_v[:, :, blk * 512:(blk + 1) * 512], "wada_d"),
                  key="wada%d" % sl)
            for kc in range(8):
                k.mm(V(PS[0][0:2, :], "ps0"), V(scv[:, kc, :], "sc"), V(wada[sl][:, kc, :], "wada%d" % sl),
                     start=(kc == 0), stop=(kc == 7))
            k.tt(V(m_sb[:, blk * 512:(blk + 1) * 512], "m_sb"), V(PS[0][0:2, :], "ps0"),
                 V(bada[:, blk * 512:(blk + 1) * 512], "bada"), ALU.add)
        for j in range(24):
            k.tr(V(PS[1][:, 2 * j:2 * j + 2], "ps1"), V(m_sb[0:2, j * 128:(j + 1) * 128], "m_sb"),
                 V(cst[0:2, C_IDENT:C_IDENT + 2], "cst"))
        k.cp(V(mT[:], "mT"), V(PS[1][:, 0:48], "ps1"))
        k.ts(V(gmod[:], "gmod"), V(mT[:, 16:32].rearrange("p (c r) -> p c r", r=2), "mT"), 1.0, None, ALU.add)
        k.tt(V(gmod[:], "gmod"), V(gmod[:], "gmod"),
             V(pp[:, PP_NORMW:PP_NORMW + 8].unsqueeze(2).to_broadcast([128, 8, 2]), "pp"), ALU.mult)
        k.memset(V(sel[:], "sel"), 0.0)
        k.memset(V(sel[0:1, 0, :], "sel"), 1.0)
        k.ts(V(sel[:, 1, :], "sel"), V(sel[:, 0, :], "sel"), -1.0, 1.0, ALU.mult, ALU.add)
        for r in range(2):
            for hb in range(2):
                k.mm(V(PS[2][:, :], "ps2"), V(sel[:, r, :], "sel"),
                     V(m_sb[0:2, 2048 + hb * 512:2048 + (hb + 1) * 512], "m_sb"), start=True)
                k.cp(V(gate_bc[:, r, hb * 512:(hb + 1) * 512], "gate_bc"), V(PS[2][:, :], "ps2"))
        k.tap("mT", V(mT[:], "mT"))
        k.tap("gate_bc", V(gate_bc[:], "gate_bc"), BF16)
        k.s.barrier()

    with ExitStack() as es1:
        xall = es1.enter_context(nc.sbuf_tensor("s_xall", [128, 16, D], F32))
        junk = es1.enter_context(nc.sbuf_tensor("s_junk", [128, D], F32))
        ss = es1.enter_context(nc.sbuf_tensor("s_ss", [128, 16], F32))
        rstd = es1.enter_context(nc.sbuf_tensor("s_rstd", [128, 16], F32))
        for tt_ in range(16):
            k.dma(V(xall[:, tt_, :], "xall%d" % tt_), V(x_d[tt_ * 128:(tt_ + 1) * 128, :], "x_d"),
                  key="x%d" % tt_)
            k.tt(V(junk[:], "junk"), V(xall[:, tt_, :], "xall%d" % tt_), V(xall[:, tt_, :], "xall%d" % tt_), ALU.mult)
            k.rsum(V(ss[:, tt_:tt_ + 1], "ss"), V(junk[:], "junk"))
        k.act(V(rstd[:], "rstd"), V(ss[:], "ss"), AF.Sqrt, bias=RMS_EPS, scale=1.0 / D)
        k.recip(V(rstd[:], "rstd"), V(rstd[:], "rstd"))
        for tt_ in range(16):
            r = 0 if tt_ < 8 else 1
            k.act(V(xall[:, tt_, :], "xall%d" % tt_), V(xall[:, tt_, :], "xall%d" % tt_), AF.Copy,
                  scale=V(rstd[:, tt_:tt_ + 1], "rstd"))
            for half in range(2):
                bank = PS[3 + half]
                bk = "ps%d" % (3 + half)
                for q in range(4):
                    kc = half * 4 + q
                    k.tr(V(bank[:, q * 128:(q + 1) * 128], bk), V(xall[:, tt_, kc * 128:(kc + 1) * 128], "xall%d" % tt_),
                         identf)
                for q in range(4):
                    kc = half * 4 + q
                    k.ts(V(hT[:, kc, tt_ * 128:(tt_ + 1) * 128], "hT%d" % tt_), V(bank[:, q * 128:(q + 1) * 128], bk),
                         V(gmod[:, kc, r:r + 1], "gmod"), V(mT[:, 2 * kc + r:2 * kc + r + 1], "mT"),
                         ALU.mult, ALU.add)
        k.tap("hT", V(hT[:], ["hT%d" % i for i in range(16)]), BF16)
        k.s.barrier()


    UaT = k.sb("UaT", [128, 8, NT], BF16)
    NTH = 1024
    def rwkv_phase():
        es2 = ExitStack()

        def sbl(name, shape, dt):
            return es2.enter_context(nc.sbuf_tensor("s_" + name, list(shape), dt))
        def pkey(b_):
            return ["ps%d" % b_, "ps%dh0" % b_, "ps%dh1" % b_] if b_ in (2, 3) else "ps%d" % b_
        NW = 6
        wsl = [sbl("wsl%d" % i, [128, 8, 128], BF16) for i in range(NW)]
        wcount = [0]

        def load_w(col0):
            sl = wcount[0] % NW
            wcount[0] += 1
            k.dma(V(wsl[sl][:], "wsl%d" % sl), V(win_v[:, :, col0:col0 + 128], "win_d"), key="wsl%d" % sl, eng="pool")
            return sl

        def proj(sl, tbs, evac):
            for tb in tbs:
                bank = tb % 2
                for kc in range(8):
                    k.mm(V(PS[bank][:, :], "ps%d" % bank), V(wsl[sl][:, kc, :], "wsl%d" % sl),
                         V(hT[:, kc, tb * 512:(tb + 1) * 512], hTk(tb)), start=(kc == 0), stop=(kc == 7))
                evac(tb, V(PS[bank][:, :], "ps%d" % bank))

        def conv_fix(z0, zk, dst, dk, ch, n, bounds):
            w0, w2 = ppc(PP_CONV + ch * 3 + 0), ppc(PP_CONV + ch * 3 + 2)
            k.stt(V(dst[:, 1:n], dk), V(z0[:, 0:n - 1], zk), w0, V(dst[:, 1:n], dk), ALU.mult, ALU.add)
            k.stt(V(dst[:, 0:n - 1], dk), V(z0[:, 1:n], zk), w2, V(dst[:, 0:n - 1], dk), ALU.mult, ALU.add)
            if bounds:
                b0, b1, st = bounds
                k.stt(V(dst[:, b0:b1:st], dk), V(z0[:, b0 - 1:b1 - 1:st], zk), ppc(PP_NEGW0 + ch),
                      V(dst[:, b0:b1:st], dk), ALU.mult, ALU.add)
                k.stt(V(dst[:, b0 - 1:b1 - 1:st], dk), V(z0[:, b0:b1:st], zk), ppc(PP_NEGW2 + ch),
                      V(dst[:, b0 - 1:b1 - 1:st], dk), ALU.mult, ALU.add)

        WA = sbl("WA", [128, NT], BF16)
        ck("pre_wa")
        LW = sbl("LW", [128, 2, 1024], BF16)
        k.dma(V(LW[:], "LW"), V(lora_d, "lora_d"), key="c6", eng="pool")
        Z0f = sbl("Z0f", [128, NT], F32)
        Z1f = sbl("Z1f", [128, NT], F32)
        slw = load_w(3072)

        def ev_wa(tb, ps):
            k.act(V(Z0f[:, tb * 512:(tb + 1) * 512], "Z0f"), ps, AF.Copy)
            k.act(V(Z1f[:, tb * 512:(tb + 1) * 512], "Z1f"), ps, AF.Copy, scale=ppc(PP_CONV + 24 * 3 + 1))
        proj(slw, range(4), ev_wa)
        conv_fix(Z0f, "Z0f", Z1f, "Z1f", 24, NT, (256, 1280, 256))
        k.act(V(WA[0:64, :], "WA"), V(Z1f[0:64, :], "Z1f"), AF.Tanh)
        k.act(V(WA[64:128, :], "WA"), V(Z1f[64:128, :], "Z1f"), AF.Copy)
        k.tap("WA", V(WA[:], "WA"), BF16)
        ck("wa")

        Z0 = Z0f
        YACC = Z1f
        RC = sbl("RC", [128, NTH], F32)
        KC = sbl("KC", [128, NTH], F32)
        VC = sbl("VC", [128, NTH], BF16)
        KK = sbl("KK", [128, NTH], F32)
        BON = sbl("BON", [128, NTH], BF16)
        AR = sbl("AR", [128, 8, 2, 128], BF16)
        BKt = sbl("BKt", [128, 8, 2, 128], BF16)
        BHT = sbl("BHT", [128, 8, 128], BF16)
        KHT = sbl("KHT", [128, 8, 128], BF16)
        VT = sbl("VT", [128, 8, 128], BF16)
        PCt = sbl("PCt", [128, 8], F32)
        RM = sbl("RM", [128, 512], BF16)
        k.memset(V(RM[:], "RM"), 1.0)
        k.memset(V(RM[:, 0:512:128], "RM"), 0.0)
        tn = ["SG", "AA", "CUM", "QB", "QC", "QD", "E1", "E2", "E3", "E4", "BA", "KD", "TMP"]
        T = {n: sbl("t_" + n, [128, 512], F32) for n in tn}
        BH = sbl("BH", [128, 512], BF16)
        KH = sbl("KH", [128, 512], BF16)
        NSET = 4
        GQ = 2
        AM = [[sbl("AM%d_%d" % (g, e), [128, 512], BF16) for e in range(2)] for g in range(NSET)]
        NTs = [sbl("NTs%d" % g, [128, 256], BF16) for g in range(NSET)]
        N2s = [sbl("N2s%d" % g, [128, 512], BF16) for g in range(NSET)]
        Zb = [sbl("Zb%d" % g, [128, 256], BF16) for g in range(NSET)]
        WTs = [sbl("WTs%d" % g, [128, 2, 64], F32) for g in range(NSET)]
        Ub = [sbl("Ub%d" % g, [128, 2, 64], BF16) for g in range(NSET)]
        UT = [sbl("UT%d" % g, [128, 128], BF16) for g in range(NSET)]
        SAT = [sbl("SAT%d" % g, [128, 128], BF16) for g in range(2)]
        STs = [sbl("ST%d" % g, [128, 64], F32) for g in range(2)]
        STb = sbl("STb", [128, 128], BF16)
        k.memset(V(STb[:], "STb"), 0.0)
        stcount = [0]
        stepcount = [0]

        def vt(name, lo=0, hi=512):
            return V(T[name][:, lo:hi], "t_" + name)

        def v3(name):
            return T[name][:].rearrange("p (n t) -> p n t", t=128)

        for c in range(8):
            sl_r, sl_k, sl_v, sl_g = load_w(c * 128), load_w(1024 + c * 128), load_w(2048 + c * 128), \
                load_w(3200 + c * 128)
            for half in range(2):
                t0 = half * NTH
                tbs = [2 * half, 2 * half + 1]
                bounds = (256, 1024, 256) if half == 0 else None
                for (slq, dst, dk, ch) in ((sl_r, RC, "RC", c), (sl_k, KC, "KC", 8 + c), (sl_v, YACC, ["YACC0", "YACC1"], 16 + c)):
                    def ev(tb, ps, dst=dst, dk=dk, ch=ch):
                        lo = (tb - 2 * half) * 512
                        k.act(V(Z0[:, lo:lo + 512], "Z0f"), ps, AF.Copy)
                        k.act(V(dst[:, lo:lo + 512], dk), ps, AF.Copy, scale=ppc(PP_CONV + ch * 3 + 1))
                    proj(slq, tbs, ev)
                    conv_fix(Z0, "Z0f", dst, dk, ch, NTH, bounds)
                k.act(V(VC[:], "VC"), V(YACC[:, 0:NTH], ["YACC0", "YACC1"]), AF.Copy)
                ck("conv")
                if c == 0:
                    k.tap("RC%d" % half, V(RC[:], "RC"))
                    k.tap("KC%d" % half, V(KC[:], "KC"))
                    k.tap("VC%d" % half, V(VC[:], "VC"), BF16)
                def kk_block(j, tA, tB, tC):
                    lo = j * 512
                    k.ts(vt(tA), V(KC[:, lo:lo + 512], "KC"), ppc(PP_KK + c), None, ALU.mult)
                    yield
                    k.tt(vt(tB), vt(tA), vt(tA), ALU.mult)
                    yield
                    k.mm(V(PS[4 + j][:, :], "ps%d" % (4 + j)), bones, vt(tB), start=True)
                    k.stt(vt(tC), V(RC[:, lo:lo + 512], "RC"), ppc(PP_RK + c), V(KC[:, lo:lo + 512], "KC"),
                          ALU.mult, ALU.mult)
                    yield
                    k.ts(vt(tB), V(PS[4 + j][:, :], "ps%d" % (4 + j)), 1e-12, None, ALU.max)
                    k.mm(V(PS[2 + j][:, :], pkey(2 + j)), bones, vt(tC), start=True)
                    yield
                    k.act(vt(tB), vt(tB), AF.Ln)
                    yield
                    k.act(vt(tB), vt(tB), AF.Exp, scale=-0.5)
                    k.tt(V(BON[:, lo:lo + 512], "BON"), V(PS[2 + j][:, :], pkey(2 + j)), V(VC[:, lo:lo + 512], "VC"),
                         ALU.mult)
                    yield
                    k.tt(V(KK[:, lo:lo + 512], "KK"), vt(tA), vt(tB), ALU.mult)
                gens = [kk_block(0, "TMP", "E1", "E2"), kk_block(1, "SG", "AA", "CUM")]
                while gens:
                    for g_ in list(gens):
                        try:
                            next(g_)
                        except StopIteration:
                            gens.remove(g_)
                for n in range(8):
                    k.tr(V(PSB[:, n * 128:(n + 1) * 128], "psb"), V(VC[:, n * 128:(n + 1) * 128], "VC"), identb)
                k.act(V(VT[:].rearrange("p n c -> p (n c)"), "VT"), V(PSB[:, :], "psb"), AF.Copy)
                if c == 0:
                    k.tap("KK%d" % half, V(KK[:], "KK"))
                ck("kk")

                for d in range(2):
                    SGs = [vt("SG"), V(Z0f[:, 0:512], "Z0f")]
                    AAs = [vt("AA"), V(Z0f[:, 512:1024], "Z0f")]
                    for j in range(2):
                        g0 = t0 + j * 512
                        pa_, pb_ = (2, 3) if j == 0 else (4, 5)
                        k.mm(V(PS[pa_][:, :], pkey(pa_)), V(LW[0:64, d, c * 128:(c + 1) * 128], "LW"),
                             V(WA[0:64, g0:g0 + 512], "WA"), start=True)
                        k.mm(V(PS[pb_][:, :], pkey(pb_)), V(LW[64:128, d, c * 128:(c + 1) * 128], "LW"),
                             V(WA[64:128, g0:g0 + 512], "WA"), start=True)
                        k.act(SGs[j], V(PS[pa_][:, :], pkey(pa_)), AF.Sigmoid, bias=ppc(PP_W0 + d * 8 + c))
                        k.act(AAs[j], V(PS[pb_][:, :], pkey(pb_)), AF.Sigmoid, bias=ppc(PP_A0 + d * 8 + c))
                    for j in range(2):
                        lo = j * 512
                        g0 = t0 + lo
                        SGv, AAv = SGs[j], AAs[j]
                        k.scan(vt("CUM"), V(RM[:, :], "RM"), SGv, 0.0, ALU.mult, ALU.add)
                        tot = V(v3("CUM")[:, :, 127:128].to_broadcast([128, 4, 128]), "t_CUM")
                        k.tt(vt("QB"), vt("CUM"), SGv, ALU.subtract)
                        k.tt(V(v3("QC"), "t_QC"), tot, V(v3("CUM"), "t_CUM"), ALU.subtract)
                        if d == 0:
                            e_in, e_ex, e_rem = "CUM", "QB", "QC"
                        else:
                            k.tt(vt("QD"), vt("QC"), SGv, ALU.add)
                            e_in, e_ex, e_rem = "QD", "QC", "QB"
                        k.act(vt("E1"), vt(e_in), AF.Exp, scale=-KAPPA)
                        k.act(vt("E2"), vt(e_ex), AF.Exp, scale=-KAPPA)
                        k.act(vt("E3"), vt(e_in), AF.Exp, scale=KAPPA)
                        k.act(vt("E4"), vt(e_rem), AF.Exp, scale=-KAPPA)
                        k.act(V(PCt[:, 4 * j:4 * j + 4], "PCt"), V(T["CUM"][:, 127:512:128], "t_CUM"), AF.Exp,
                              scale=-KAPPA)
                        k.tt(vt("BA"), V(KK[:, lo:lo + 512], "KK"), AAv, ALU.mult)
                        k.ts(vt("TMP"), AAv, ppc(PP_KA + c), ppc(PP_OMKA + c), ALU.mult, ALU.add)
                        k.tt(vt("KD"), V(KC[:, lo:lo + 512], "KC"), vt("TMP"), ALU.mult)
                        ns = slice(4 * j, 4 * j + 4)
                        k.stt(V(AR[:, ns, 0, :], "AR"), V(KK[:, lo:lo + 512].rearrange("p (n t) -> p n t", t=128), "KK"),
                              -1.0, V(v3("E2"), "t_E2"), ALU.mult, ALU.mult)
                        k.tt(V(AR[:, ns, 1, :], "AR"), V(RC[:, lo:lo + 512].rearrange("p (n t) -> p n t", t=128), "RC"),
                             V(v3("E1"), "t_E1"), ALU.mult)
                        k.tt(V(BKt[:, ns, 0, :], "BKt"), V(v3("BA"), "t_BA"), V(v3("E3"), "t_E3"), ALU.mult)
                        k.tt(V(BKt[:, ns, 1, :], "BKt"), V(v3("KD"), "t_KD"), V(v3("E3"), "t_E3"), ALU.mult)
                        k.tt(V(BH[:], "BH"), vt("BA"), vt("E4"), ALU.mult)
                        k.tt(V(KH[:], "KH"), vt("KD"), vt("E4"), ALU.mult)
                        for n in range(4):
                            k.tr(V(PSB[:, n * 128:(n + 1) * 128], "psb"), V(BH[:, n * 128:(n + 1) * 128], "BH"), identb)
                            k.tr(V(PSB[:, 512 + n * 128:512 + (n + 1) * 128], "psb"),
                                 V(KH[:, n * 128:(n + 1) * 128], "KH"), identb)
                        k.act(V(BHT[:, ns, :].rearrange("p n c -> p (n c)"), "BHT"), V(PSB[:, 0:512], "psb"), AF.Copy)
                        k.act(V(KHT[:, ns, :].rearrange("p n c -> p (n c)"), "KHT"), V(PSB[:, 512:1024], "psb"), AF.Copy)
                        if c == 0 and half == 0 and j == 0:
                            k.tap("SG%d" % d, SGv)
                            k.tap("AA%d" % d, AAv)
                            k.tap("CUM%d" % d, vt("CUM"))
                    if c == 0:
                        k.tap("AR%d%d" % (half, d), V(AR[:], "AR"), BF16)
                        k.tap("BKt%d%d" % (half, d), V(BKt[:], "BKt"), BF16)
                        k.tap("BHT%d%d" % (half, d), V(BHT[:], "BHT"), BF16)
                        k.tap("PCt%d%d" % (half, d), V(PCt[:], "PCt"))

                    ck("prep")
                    if half == 0:
                        seqs = [[2 * q, 2 * q + 1] for q in range(4)]
                    else:
                        seqs = [list(range(8))]
                    items = []
                    for qi, chs in enumerate(seqs):
                        if d == 1:
                            chs = chs[::-1]
                        for ii, n in enumerate(chs):
                            items.append((n, qi, ii == 0, ii == len(chs) - 1))
                    mxk = C_MX0 if d == 0 else C_MX1
                    mtk = C_MT0 if d == 0 else C_MT1
                    maskx = V(cstb[:, mxk:mxk + 512], "cstb")
                    maskt = V(cstb[:, mtk:mtk + 128], "cstb")

                    def pre_pair(pi):
                        its = items[GQ * pi:GQ * pi + GQ]
                        gs = [(GQ * pi + q) % NSET for q in range(GQ)]
                        for q, (n, _, _, _) in enumerate(its):
                            g = gs[q]
                            n2b, zb_, zc = 2 + q, 4 + q, 0
                            for e in range(2):
                                pr = slice(64 * e, 64 * e + 64)
                                pk = "ps%d" % e
                                k.mm(V(PS[e][:, 0:256], pk), V(BKt[pr, n, 0, :], "BKt"),
                                     V(AR[pr, n, :, :].rearrange("p a t -> p (a t)"), "AR"), start=True)
                                k.mm(V(PS[e][:, 256:512], pk), V(BKt[pr, n, 1, :], "BKt"),
                                     V(AR[pr, n, :, :].rearrange("p a t -> p (a t)"), "AR"), start=False)
                                nb_ = n2b if e == 0 else zb_
                                ntb = PS[nb_]
                                ntk = pkey(nb_)
                                ntc = 0 if e == 0 else 256
                                k.mm(V(ntb[:, ntc:ntc + 128], ntk), V(AR[pr, n, 0, :], "AR"), V(BKt[pr, n, 0, :], "BKt"),
                                     start=True)
                                k.tt(V(AM[g][e][:], "AM%d_%d" % (g, e)), V(PS[e][:, :], pk), maskx, ALU.mult)
                                k.tt(V(NTs[g][:, e * 128:(e + 1) * 128], "NTs%d" % g), V(ntb[:, ntc:ntc + 128], ntk),
                                     maskt, ALU.mult)
                            ck("pa")
                            zk_ = "ps%d" % zb_
                            for e in range(2):
                                k.mm(V(PS[zb_][:, zc + e * 128:zc + e * 128 + 64], zk_), V(AM[g][e][:, 256:384], "AM%d_%d" % (g, e)),
                                     V(VT[:, n, 64 * e:64 * e + 64], "VT"), start=(e == 0))
                                k.mm(V(PS[zb_][:, zc + e * 128 + 64:zc + (e + 1) * 128], zk_), V(AR[:, n, 0, :], "AR"),
                                     V(cstb[:, C_IDENT + 64 * e:C_IDENT + 64 * e + 64], "cstb"), start=False)
                        ck("pz")
                        yield
                        ncur = [[V(AM[gs[q]][e][:, 0:128], "AM%d_%d" % (gs[q], e)) for e in range(2)] for q in range(GQ)]
                        ntcur = [[V(NTs[gs[q]][:, e * 128:(e + 1) * 128], "NTs%d" % gs[q]) for e in range(2)]
                                 for q in range(GQ)]
                        for lev in range(7):
                            for q in range(len(its)):
                                g = gs[q]
                                n2b, zb_, zc = 2 + q, 4 + q, 0
                                zk_ = "ps%d" % zb_
                                k.act(V(Zb[g][:], "Zb%d" % g), V(PS[zb_][:, 0:256], zk_), AF.Copy)
                                for e in range(2):
                                    k.mm(V(PS[zb_][:, e * 128:(e + 1) * 128], zk_), ncur[q][e],
                                         V(Zb[g][:, e * 128:(e + 1) * 128], "Zb%d" % g), start=False)
                                if lev < 6:
                                    for e in range(2):
                                        hk_ = "ps%dh%d" % (n2b, e)
                                        k.mm(V(PS[n2b][:, e * 256:e * 256 + 128], hk_), ntcur[q][e], ncur[q][e],
                                             start=(e == 0))
                                        if lev < 5:
                                            k.mm(V(PS[n2b][:, e * 256 + 128:(e + 1) * 256], hk_), ncur[q][e], ntcur[q][e],
                                                 start=False)
                                            k.cp(V(N2s[g][:, e * 256:(e + 1) * 256], "N2s%d_%d" % (g, e)),
                                                 V(PS[n2b][:, e * 256:(e + 1) * 256], hk_))
                                        else:
                                            k.cp(V(N2s[g][:, e * 256:e * 256 + 128], "N2s%d_%d" % (g, e)),
                                                 V(PS[n2b][:, e * 256:e * 256 + 128], hk_))
                                    ncur[q] = [V(N2s[g][:, e * 256:e * 256 + 128], "N2s%d_%d" % (g, e)) for e in range(2)]
                                    ntcur[q] = [V(N2s[g][:, e * 256 + 128:(e + 1) * 256], "N2s%d_%d" % (g, e))
                                                for e in range(2)]
                            yield
                        ck("pl")
                        for q in range(len(its)):
                            g = gs[q]
                            zb_, zc = 4 + q, 0
                            zk_ = "ps%d" % zb_
                            zv = PS[zb_][:, zc:zc + 256].rearrange("p (e x) -> p e x", e=2)
                            k.act(V(WTs[g][:], "WTs%d" % g), V(zv[:, :, 0:64], zk_), AF.Copy)
                            ck("f1")
                            k.act(V(Ub[g][:], "Ub%d" % g), V(zv[:, :, 64:128], zk_), AF.Copy)
                            ck("f2")
                            k.tr(V(PSB[:, q * 128:(q + 1) * 128], "psb"),
                                 V(Ub[g][:].rearrange("p e x -> p (e x)"), "Ub%d" % g), identb)
                            ck("f3")
                            k.act(V(UT[g][:], "UT%d" % g), V(PSB[:, q * 128:(q + 1) * 128], "psb"), AF.Copy)

                    def step_pair(pi):
                        its = items[GQ * pi:GQ * pi + GQ]
                        for q, (n, qi, is_start, is_end) in enumerate(its):
                            g = (GQ * pi + q) % NSET
                            if is_start:
                                stcount[0] += 1
                            sti = stcount[0] % 2
                            ST = STs[sti]
                            stk = "ST%d" % sti
                            if is_start:
                                if half == 0:
                                    k.memset(V(ST[:], stk), 0.0)
                                else:
                                    k.dma(V(ST[:], stk), V(st0_d[c, d], "st0_d"), key="st%d" % sti)
                            k.act(V(STb[0:64, 0:64], "STb"), V(ST[0:64, :], stk), AF.Copy)
                            k.act(V(STb[64:128, 64:128], "STb"), V(ST[64:128, :], stk), AF.Copy)
                            yield
                            sg_ = stepcount[0] % 2
                            stepcount[0] += 1
                            sat = SAT[sg_]
                            satk = "SAT%d" % sg_
                            B6 = PS[6]
                            k.mm(V(B6[:, 0:128], "ps6"), V(UT[g][:], "UT%d" % g), V(STb[:], "STb"), start=True)
                            k.tt(V(sat[:], satk), V(B6[:, 0:128], "ps6"),
                                 V(WTs[g][:].rearrange("p e x -> p (e x)"), "WTs%d" % g), ALU.add)
                            yield
                            for e in range(2):
                                pr = slice(64 * e, 64 * e + 64)
                                cs = slice(64 * e, 64 * e + 64)
                                k.mm(V(B6[pr, 128:192], "ps6"), V(BHT[:, n, cs], "BHT"), V(sat[:, cs], satk), start=False)
                                k.mm(V(B6[pr, 128:192], "ps6"), V(KHT[:, n, cs], "KHT"), V(VT[:, n, cs], "VT"), start=False)
                            k.mm(V(B6[:, 192:320], "ps6"), V(STb[:], "STb"), V(AR[:, n, 1, :], "AR"), start=False)
                            for e in range(2):
                                pr = slice(64 * e, 64 * e + 64)
                                cs = slice(64 * e, 64 * e + 64)
                                amk = "AM%d_%d" % (g, e)
                                k.mm(V(B6[pr, 192:320], "ps6"), V(sat[:, cs], satk), V(AM[g][e][:, 128:256], amk), start=False)
                                k.mm(V(B6[pr, 192:320], "ps6"), V(VT[:, n, cs], "VT"), V(AM[g][e][:, 384:512], amk), start=False)
                            yield
                            k.stt(V(ST[:], stk), V(ST[:], stk), V(PCt[:, n:n + 1], "PCt"), V(B6[:, 128:192], "ps6"),
                                  ALU.mult, ALU.add)
                            if d == 0:
                                k.act(V(YACC[:, n * 128:(n + 1) * 128], "YACC%d" % (n // 4)), V(B6[:, 192:320], "ps6"), AF.Copy)
                            else:
                                k.tt(V(YACC[:, n * 128:(n + 1) * 128], "YACC%d" % (n // 4)), V(B6[:, 192:320], "ps6"),
                                     V(YACC[:, n * 128:(n + 1) * 128], "YACC%d" % (n // 4)), ALU.add)
                            if is_end and half == 0:
                                k.dma(V(newsT_d[qi, d, c], "news_d"), V(ST[:], stk), key="stout%d" % sti)

                    npairs = len(items) // GQ
                    for _ in pre_pair(0):
                        pass
                    ck("pre0")
                    for pi in range(npairs):
                        gens = []
                        if pi + 1 < npairs:
                            gens.append(pre_pair(pi + 1))
                        gens.append(step_pair(pi))
                        while gens:
                            for g_ in list(gens):
                                try:
                                    next(g_)
                                except StopIteration:
                                    gens.remove(g_)
                ck("scan")
                if c == 0:
                    k.tap("YACC%d" % half, V(YACC[:, 0:NTH], ["YACC0", "YACC1"]))
                def fin_block(j, tE):
                    lo = j * 512
                    yv = V(YACC[:, lo:lo + 512], "YACC%d" % j)
                    pa, pb = 4 + j, 2 + j
                    k.mm(V(PS[pa][:, :], "ps%d" % pa), bones, yv, start=True)
                    yield
                    k.stt(yv, V(PS[pa][:, :], "ps%d" % pa), -1.0 / 64, yv, ALU.mult, ALU.add)
                    yield
                    k.tt(vt(tE), yv, yv, ALU.mult)
                    yield
                    k.mm(V(PS[pb][:, :], pkey(pb)), bones, vt(tE), start=True)
                    yield
                    k.act(vt(tE), V(PS[pb][:, :], pkey(pb)), AF.Ln, bias=ppc(PP_GNEPS), scale=1.0 / 64)
                    yield
                    k.act(vt(tE), vt(tE), AF.Exp, scale=-0.5)
                    yield
                    k.tt(yv, yv, vt(tE), ALU.mult)
                    yield
                    k.ts(yv, yv, ppc(PP_LNW + c), ppc(PP_LNB + c), ALU.mult, ALU.add)
                    yield
                    k.tt(yv, yv, V(BON[:, lo:lo + 512], "BON"), ALU.add)
                gens = [fin_block(0, "E1"), fin_block(1, "E3")]
                while gens:
                    for g_ in list(gens):
                        try:
                            next(g_)
                        except StopIteration:
                            gens.remove(g_)

                def ev_g(tb, ps):
                    lo = (tb - 2 * half) * 512
                    k.act(vt("E2"), ps, AF.Silu)
                    k.tt(V(UaT[:, c, tb * 512:(tb + 1) * 512], "UaT%d" % tb), V(YACC[:, lo:lo + 512], "YACC%d" % (lo // 512)), vt("E2"),
                         ALU.mult)
                proj(sl_g, tbs, ev_g)
                ck("fin")
        k.tap("UaT", V(UaT[:], ["UaT%d" % i for i in range(4)]), BF16)
        k.s.barrier()
        es2.close()

    if "rwkv" in phases:
        try:
            rwkv_phase()
        except StopBuild:
            k.s.barrier()


    UbT = k.sb("UbT", [128, 8, NT], BF16) if ("attn" in phases or "merge" in phases) else None
    QOFF, KOFF, VOFF, GBOFF, MAOFF, MBOFF = 4224, 5248, 5504, 5760, 6784, 7808

    MNW = 8
    m_wsl = [k.sb("m_wsl%d" % i, [128, 8, 128], BF16) for i in range(MNW)] if "merge" in phases else None
    WOUT = k.sb("WOUT", [128, 8, D], BF16) if "merge" in phases else None
    m_wcount = [0]
    woa_v = woa_d.rearrange("(kc p) n -> p kc n", p=128)
    wob_v = wob_d.rearrange("(kc p) n -> p kc n", p=128)
    wout_v = wout_d.rearrange("(kc p) n -> p kc n", p=128)

    def m_load_w(src_v, col0):
        sl = m_wcount[0] % MNW
        m_wcount[0] += 1
        k.dma(V(m_wsl[sl][:, :, :], "m_wsl%d" % sl), V(src_v[:, :, col0:col0 + 128], "w_d"), key="m_wsl%d" % sl, eng="pool")
        return sl

    def m_loads(dc):
        return (m_load_w(win_v, MAOFF + dc * 128), m_load_w(win_v, MBOFF + dc * 128),
                m_load_w(woa_v, dc * 128), m_load_w(wob_v, dc * 128))
    m_pref = {}

    def merge_prefetch():
        if "merge" not in phases:
            return
        for dc in range(2):
            m_pref[dc] = m_loads(dc)
        for hb in range(2):
            k.dma(V(WOUT[:, :, hb * 512:(hb + 1) * 512], "WOUT"), V(wout_v[:, :, hb * 512:(hb + 1) * 512], "wout_d"),
                  key="WOUT", eng="pool")

    def attn_phase():
        es3 = ExitStack()

        def sbl(name, shape, dt):
            return es3.enter_context(nc.sbuf_tensor("s3_" + name, list(shape), dt))
        NW = 4
        wsl = [sbl("wsl%d" % i, [128, 8, 128], BF16) for i in range(NW)]
        wcount = [0]

        def load_w(col0, dup64=False):
            sl = wcount[0] % NW
            wcount[0] += 1
            if dup64:
                for h_ in range(2):
                    k.dma(V(wsl[sl][:, :, 64 * h_:64 * h_ + 64], "a_wsl%d" % sl), V(win_v[:, :, col0:col0 + 64], "win_d"),
                          key="a_wsl%d" % sl, eng="pool")
            else:
                k.dma(V(wsl[sl][:, :, :], "a_wsl%d" % sl), V(win_v[:, :, col0:col0 + 128], "win_d"),
                      key="a_wsl%d" % sl, eng="pool")
            return sl
        COS = sbl("COS", [128, 1024], F32)
        SIN = sbl("SIN", [128, 1024], F32)
        k.dma(V(COS[:], "COS"), V(rope_d[0], "rope_d"), key="c7")
        k.dma(V(SIN[:], "SIN"), V(rope_d[1], "rope_d"), key="c8")
        perm = V(cstb[:, C_PERM:C_PERM + 128], "cstb")
        esink = sbl("esink", [128, 16], F32)
        k.act(V(esink[:], "esink"), V(pp[:, PP_SINK:PP_SINK + 16], "pp"), AF.Exp)
        VTK = sbl("VTK", [128, 16, 4, 66], BF16)
        KCT = sbl("KCT", [128, 4, 256], BF16)
        CV = sbl("CV", [128, 2, 4, 66], BF16)
        es3b = ExitStack()

        def sblb(name, shape, dt):
            return es3b.enter_context(nc.sbuf_tensor("s3_" + name, list(shape), dt))
        WKV = sblb("WKV", [128, 8, 512], BF16)
        k.dma(V(WKV[:], "WKV"), V(win_v[:, :, KOFF:KOFF + 512], "win_d"), key="WKV", eng="pool")
        k.memset(V(VTK[:, :, :, 64:65], "VTKones"), 1.0)
        k.memset(V(CV[:, :, :, 64:65], "CVones"), 1.0)
        for blk in range(2):
            k.dma(V(CV[:, blk, :, 0:64], "CV"),
                  V(cachev_d[:, blk * 128:(blk + 1) * 128, :].rearrange("h t d -> t h d"), "cv_d"),
                  key="c9", eng="pool")
        CK2 = sblb("CK2", [128, 2, 4, 2, 64], BF16)
        for dup in range(2):
            for blk in range(2):
                k.dma(V(CK2[:, blk, :, dup, :], "CK2"),
                      V(cachek_d[:, blk * 128:(blk + 1) * 128, :].rearrange("h t d -> t h d"), "ck_d"),
                      key="c10", eng="pool")
        for hk in range(4):
            for blk in range(2):
                k.tr(V(PSB[:, (hk * 2 + blk) * 128:(hk * 2 + blk + 1) * 128], "psb"),
                     V(CK2[:, blk, hk, :, :].rearrange("p a d -> p (a d)"), "CK2"), identb)
        k.act(V(KCT[:].rearrange("p h t -> p (h t)"), "KCT"), V(PSB[:, :], "psb"), AF.Copy)
        stg = [sblb("kvstg%d" % i, [128, 512], F32) for i in range(2)]
        for t in range(16):
            b_ = t % 2
            bk = "ps%d" % b_
            for kc in range(8):
                k.mm(V(PS[b_][:, :], bk), V(hT[:, kc, t * 128:(t + 1) * 128], "hT%d" % t), V(WKV[:, kc, :], "WKV"),
                     start=(kc == 0), stop=(kc == 7))
            k.act(V(stg[b_][:], "kvstg%d" % b_), V(PS[b_][:, :], bk), AF.Copy)
            k.cp(V(VTK[:, t, :, 0:64], "VTK%d" % t), V(stg[b_][:, 256:512].rearrange("p (h d) -> p h d", d=64), "kvstg%d" % b_))
            if t < 8:
                sq, hf = t // 2, t % 2
                k.dma(V(newk_d[sq, :, hf * 128:(hf + 1) * 128, :].rearrange("h t d -> t h d"), "newk_d"),
                      V(stg[b_][:, 0:256].rearrange("p (h d) -> p h d", d=64), "kvstg%d" % b_), key="kvout%d" % b_)
                k.dma(V(newv_d[sq, :, hf * 128:(hf + 1) * 128, :].rearrange("h t d -> t h d"), "newv_d"),
                      V(stg[b_][:, 256:512].rearrange("p (h d) -> p h d", d=64), "kvstg%d" % b_), key="kvout%d" % b_)
        ck("kv")
        k.s.barrier()
        es3b.close()
        QT = [sbl("QT%d" % i, [128, NT], BF16) for i in range(2)]
        KT2 = sbl("KT2", [128, NT], BF16)
        WG = sbl("WG", [128, 8, 256], BF16)
        QB16 = sbl("QB16", [128, 512], BF16)
        T1 = sbl("T1", [128, 512], F32)
        T2 = sbl("T2", [128, 512], F32)
        Et = [sbl("E%d" % i, [128, 512], BF16) for i in range(3)]
        DEN = sbl("DEN", [128, 4], F32)
        YBt = sbl("YBt", [128, 256], F32)
        SGall = sbl("SGall", [128, 16, 256], BF16)
        UBt = sbl("UBt", [128, 256], BF16)
        ecount = [0]
        pending = [None]
        for hk in range(4):
            slq = [load_w(QOFF + (2 * hk + i) * 128) for i in range(2)]
            slk = load_w(KOFF + hk * 64, dup64=True)
            k.dma(V(WG[:], "WG"), V(win_v[:, :, GBOFF + hk * 256:GBOFF + (hk + 1) * 256], "win_d"), key="WG", eng="pool")
            ck("aload")
            for (sl, dst, dk) in ((slq[0], QT[0], "QT0"), (slq[1], QT[1], "QT1"), (slk, KT2, "KT2")):
                for tb in range(4):
                    if tb == 2:
                        ck("aq01")
                    if tb == 3:
                        ck("aq2")
                    bank = tb % 2
                    bk = "ps%d" % bank
                    for kc in range(8):
                        k.mm(V(PS[bank][:, :], bk), V(wsl[sl][:, kc, :], "a_wsl%d" % sl),
                             V(hT[:, kc, tb * 512:(tb + 1) * 512], hTk(tb)), start=(kc == 0), stop=(kc == 7))
                    ps = V(PS[bank][:, :], bk)
                    dv = V(dst[:, tb * 512:(tb + 1) * 512], dk)
                    if tb < 2:
                        k.act(dv, ps, AF.Copy)
                    else:
                        s0 = (tb - 2) * 512
                        k.act(V(QB16[:], "QB16"), ps, AF.Copy)
                        k.act(V(T1[:], "T1"), ps, AF.Copy)
                        k.tt(V(T1[:], "T1"), V(T1[:], "T1"), V(COS[:, s0:s0 + 512], "COS"), ALU.mult)
                        k.mm(V(PS[6][:, :], "ps6"), perm, V(QB16[:], "QB16"), start=True)
                        k.act(V(T2[:], "T2"), V(PS[6][:, :], "ps6"), AF.Copy)
                        k.tt(V(T2[:], "T2"), V(T2[:], "T2"), V(SIN[:, s0:s0 + 512], "SIN"), ALU.mult)
                        k.tt(dv, V(T1[:], "T1"), V(T2[:], "T2"), ALU.add)
            ck("aproj")
            for t in range(16):
                gbk = 6 if t % 2 == 0 else 3
                for kc in range(8):
                    k.mm(V(PS[gbk][:, 0:256], "ps%d" % gbk), V(hT[:, kc, t * 128:(t + 1) * 128], "hT%d" % t),
                         V(WG[:, kc, :], "WG"), start=(kc == 0), stop=(kc == 7))
                k.act(V(SGall[:, t, :], "SGall"), V(PS[gbk][:, 0:256], "ps%d" % gbk), AF.Silu)
            if hk == 0:
                k.tap("QT0", V(QT[0][:], "QT0"), BF16)
                k.tap("KT2", V(KT2[:], "KT2"), BF16)
            for t in range(16):
                kbs = []
                if t < 8:
                    sq = t // 2
                    for kt in (2 * sq, 2 * sq + 1):
                        kbs.append(("loc", kt, None))
                else:
                    i = t - 8
                    if i - 1 >= 0:
                        kbs.append(("loc", t - 1, C_TRI_GE))
                    kbs.append(("loc", t, None))
                    if i + 1 < 8:
                        kbs.append(("loc", t + 1, C_TRI_LE))
                    kbs.append(("ctx", 0, None))
                    kbs.append(("ctx", 1, None))
                ob = 4 + (t % 2)
                obk = "ps%d" % ob
                def scores(ki):
                    kind, kt, msk = kbs[ki]
                    sb_ = 2 * (ki % 2)
                    E = Et[ecount[0] % 3]
                    ek = "E%d" % (ecount[0] % 3)
                    ecount[0] += 1
                    Ev = E[:].rearrange("p (c e q) -> p c e q", c=2, e=2)
                    for e in range(2):
                        pr = slice(64 * e, 64 * e + 64)
                        bk = "ps%d" % (sb_ + e)
                        for cl in range(2):
                            if kind == "loc":
                                lhs = V(KT2[pr, kt * 128:(kt + 1) * 128], "KT2")
                            else:
                                lhs = V(KCT[pr, hk, kt * 128:(kt + 1) * 128], "KCT")
                            k.mm(V(PS[sb_ + e][:, cl * 128:(cl + 1) * 128], bk), lhs,
                                 V(QT[cl][pr, t * 128:(t + 1) * 128], "QT%d" % cl), start=(cl == 0))
                        k.act(V(Ev[:, :, e, :], ek), V(PS[sb_ + e][:, 0:256].rearrange("p (c q) -> p c q", c=2), bk),
                              AF.Exp, scale=0.125)
                    if msk is not None:
                        mv = V(cstb[:, msk:msk + 128].unsqueeze(1).to_broadcast([128, 4, 128]), "cstb")
                        k.tt(V(E[:].rearrange("p (h q) -> p h q", h=4), ek), V(E[:].rearrange("p (h q) -> p h q", h=4), ek),
                             mv, ALU.mult)
                    return Ev, ek

                def pv(ki, Ev, ek):
                    kind, kt, msk = kbs[ki]
                    for cl in range(2):
                        for e in range(2):
                            h4 = cl * 2 + e
                            if kind == "loc":
                                rhs = V(VTK[:, kt, hk, 0:65], ["VTK%d" % kt, "VTKones"])
                            else:
                                rhs = V(CV[:, kt, hk, 0:65], ["CV", "CVones"])
                            k.mm(V(PS[ob][:, h4 * 66:h4 * 66 + 65], obk), V(Ev[:, cl, e, :], ek), rhs,
                                 start=(ki == 0 and h4 == 0))
                cur = scores(0)
                for ki in range(len(kbs)):
                    nxt = scores(ki + 1) if ki + 1 < len(kbs) else None
                    pv(ki, *cur)
                    cur = nxt
                def finalize(t=t, ob=ob, obk=obk, hk=hk):
                    ov = PS[ob][:, 0:264].rearrange("p (h x) -> p h x", x=66)
                    k.tt(V(DEN[:], "DEN"), V(ov[:, :, 64], obk), V(esink[:, 4 * hk:4 * hk + 4], "esink"), ALU.add)
                    k.recip(V(DEN[:], "DEN"), V(DEN[:], "DEN"))
                    k.tt(V(YBt[:].rearrange("p (h d) -> p h d", d=64), "YBt"), V(ov[:, :, 0:64], obk),
                         V(DEN[:].unsqueeze(2).to_broadcast([128, 4, 64]), "DEN"), ALU.mult)
                    k.tt(V(UBt[:], "UBt"), V(YBt[:], "YBt"), V(SGall[:, t, :], "SGall"), ALU.mult)
                    for i in range(2):
                        k.tr(V(PSB[:, i * 128:(i + 1) * 128], "psb"), V(UBt[:, i * 128:(i + 1) * 128], "UBt"), identb)
                    k.act(V(UbT[:, 2 * hk:2 * hk + 2, t * 128:(t + 1) * 128], "UbT%d" % (t // 4)),
                          V(PSB[:, 0:256].rearrange("p (i q) -> p i q", i=2), "psb"), AF.Copy)

                if pending[0] is not None:
                    pending[0]()
                pending[0] = finalize
            if pending[0] is not None:
                pending[0]()
                pending[0] = None
        k.tap("UbT", V(UbT[:], ["UbT%d" % i for i in range(4)]), BF16)
        merge_prefetch()
        k.s.barrier()
        es3.close()

    if "attn" in phases:
        try:
            attn_phase()
        except StopBuild:
            k.s.barrier()

    def merge_phase():
        es4 = ExitStack()

        def sbl(name, shape, dt):
            return es4.enter_context(nc.sbuf_tensor("s4_" + name, list(shape), dt))
        wsl = m_wsl
        if not m_pref:
            merge_prefetch()
        FNW = sbl("FNW", [128, D], F32)
        k.dma(V(FNW[:], "FNW"), V(fnw_d, "fnw_d"), key="c11")
        mergedT = sbl("mergedT", [128, 8, NT], BF16)
        SMA = sbl("SMA", [128, 512], F32)
        SMB = sbl("SMB", [128, 512], F32)
        TA = sbl("TA", [128, 512], F32)
        TB = sbl("TB", [128, 512], F32)
        xt = [sbl("xt%d" % i, [128, D], F32) for i in range(2)]
        xs = [sbl("xs%d" % i, [128, D], F32) for i in range(2)]
        junk = sbl("junk", [128, D], F32)
        ssq = sbl("ssq", [128, 16], F32)
        for half in range(1):
            for dc in range(8):
                if dc + 1 < 8 and (dc + 1) not in m_pref:
                    m_pref[dc + 1] = m_loads(dc + 1)
                s_ma, s_mb, s_oa, s_ob = m_pref[dc]
                for j in range(4):
                    tb = j
                    tsl = slice(tb * 512, (tb + 1) * 512)
                    for (sl, src, srck, bank) in ((s_ma, hT, hTk(tb), 0), (s_mb, hT, hTk(tb), 1),
                                                  (s_oa, UaT, ["UaT%d" % tb], 2), (s_ob, UbT, ["UbT%d" % tb], 3)):
                        for kc in range(8):
                            k.mm(V(PS[bank][:, :], "ps%d" % bank), V(wsl[sl][:, kc, :], "m_wsl%d" % sl),
                                 V(src[:, kc, tsl], srck), start=(kc == 0), stop=(kc == 7))
                    k.act(V(SMA[:], "SMA"), V(PS[0][:, :], "ps0"), AF.Sigmoid)
                    k.act(V(SMB[:], "SMB"), V(PS[1][:, :], "ps1"), AF.Sigmoid)
                    k.tt(V(TA[:], "TA"), V(PS[2][:, :], "ps2"), V(SMA[:], "SMA"), ALU.mult)
                    k.tt(V(TB[:], "TB"), V(PS[3][:, :], "ps3"), V(SMB[:], "SMB"), ALU.mult)
                    k.tt(V(mergedT[:, dc, j * 512:(j + 1) * 512], "mergedT"), V(TA[:], "TA"), V(TB[:], "TB"), ALU.add)
            if half == 0:
                k.tap("mergedT", V(mergedT[:], "mergedT"), BF16)
            for tl in range(16):
                t = tl
                r = 0 if t < 8 else 1
                xb = t % 2
                k.dma(V(xt[xb][:], "xt%d" % xb), V(x_d[t * 128:(t + 1) * 128, :], "x_d"), key="xt%d" % xb)
                for nb in range(2):
                    bank = 4 + nb
                    for kc in range(8):
                        k.mm(V(PS[bank][:, :], "ps%d" % bank), V(mergedT[:, kc, tl * 128:(tl + 1) * 128], "mergedT"),
                             V(WOUT[:, kc, nb * 512:(nb + 1) * 512], "WOUT"), start=(kc == 0), stop=(kc == 7))
                    xv = V(xs[xb][:, nb * 512:(nb + 1) * 512], "xs%d" % xb)
                    k.tt(xv, V(PS[bank][:, :], "ps%d" % bank), V(gate_bc[:, r, nb * 512:(nb + 1) * 512], "gate_bc"), ALU.mult)
                    k.tt(xv, xv, V(xt[xb][:, nb * 512:(nb + 1) * 512], "xt%d" % xb), ALU.add)
                k.tt(V(junk[:], "junk4"), V(xs[xb][:], "xs%d" % xb), V(xs[xb][:], "xs%d" % xb), ALU.mult)
                k.rsum(V(ssq[:, t:t + 1], "ssq%d" % t), V(junk[:], "junk4"))
                k.act(V(ssq[:, t:t + 1], "ssq%d" % t), V(ssq[:, t:t + 1], "ssq%d" % t), AF.Sqrt, bias=RMS_EPS, scale=1.0 / D)
                k.recip(V(ssq[:, t:t + 1], "ssq%d" % t), V(ssq[:, t:t + 1], "ssq%d" % t))
                k.act(V(xs[xb][:], "xs%d" % xb), V(xs[xb][:], "xs%d" % xb), AF.Copy, scale=V(ssq[:, t:t + 1], "ssq%d" % t))
                k.tt(V(xs[xb][:], "xs%d" % xb), V(xs[xb][:], "xs%d" % xb), V(FNW[:], "FNW"), ALU.mult)
                k.dma(V(y_d[t * 128:(t + 1) * 128, :], "y_d"), V(xs[xb][:], "xs%d" % xb), key="yout%d" % xb)
        k.s.barrier()
        es4.close()

    if "merge" in phases:
        try:
            merge_phase()
        except StopBuild:
            k.s.barrier()

    k.emit()
    return nc, es, k


def host_layout(inputs, core):
    f = lambda a: np.ascontiguousarray(np.asarray(a, dtype=np.float32))
    b = core % 4
    m = {}
    xp = f(inputs["x_prompt"])[4 * core:4 * core + 4].reshape(1024, D)
    xs = f(inputs["x_sample"])[b]
    m["x"] = np.ascontiguousarray(np.concatenate([xp, xs], 0))
    cond = np.stack([f(inputs["c_ctx"]), f(inputs["c"])[b]], 0)
    m["condT"] = np.ascontiguousarray(cond.reshape(2, 8, 128).transpose(2, 1, 0).reshape(128, 16))
    m["w_ada"] = f(inputs["w_ada"])[0]
    m["bada2"] = np.ascontiguousarray(np.tile(f(inputs["b_ada"])[0][None], (2, 1)))
    m["w_in"] = f(inputs["w_in"])[0]
    pp = np.zeros((128, PP_N), np.float32)

    def fm(v, n):
        return v.reshape(n, 128).T
    pp[:, PP_NORMW:PP_NORMW + 8] = fm(f(inputs["norm_w"])[0], 8)
    ca = f(inputs["conv_a"])[0]
    pp[:, PP_CONV:PP_CONV + 75] = ca.reshape(3, 25, 128).transpose(2, 1, 0).reshape(128, 75)
    pp[:, PP_KK:PP_KK + 8] = fm(f(inputs["k_k"])[0], 8)
    pp[:, PP_KA:PP_KA + 8] = fm(f(inputs["k_a"])[0], 8)
    pp[:, PP_RK:PP_RK + 8] = fm(f(inputs["r_k"])[0].reshape(-1), 8)
    pp[:, PP_LNW:PP_LNW + 8] = fm(f(inputs["ln_x_w"])[0], 8)
    pp[:, PP_LNB:PP_LNB + 8] = fm(f(inputs["ln_x_b"])[0], 8)
    pp[:, PP_W0:PP_W0 + 16] = f(inputs["w0"])[0].reshape(2, 8, 128).transpose(2, 0, 1).reshape(128, 16)
    pp[:, PP_A0:PP_A0 + 16] = f(inputs["a0"])[0].reshape(2, 8, 128).transpose(2, 0, 1).reshape(128, 16)
    pp[:, PP_SINK:PP_SINK + 16] = f(inputs["sink"])[0].reshape(1, 16)
    m["pp"] = pp
    lora = np.concatenate([f(inputs["w_up"])[0], f(inputs["a_up"])[0]], 1)
    m["lora"] = np.ascontiguousarray(lora.transpose(1, 0, 2))
    st = f(inputs["state_rwkv"])[b, 0]
    m["st0"] = np.ascontiguousarray(st.reshape(2, 8, 2, 64, 64).transpose(1, 0, 2, 4, 3).reshape(8, 2, 128, 64))
    m["cst"] = make_consts()
    m["rope"] = make_rope()
    m["cache_k"] = np.ascontiguousarray(f(inputs["cache_k"])[b, 0])
    m["cache_v"] = np.ascontiguousarray(f(inputs["cache_v"])[b, 0])
    m["w_oA"] = f(inputs["w_oA"])[0]
    m["w_oB"] = f(inputs["w_oB"])[0]
    m["w_out"] = f(inputs["w_out"])[0]
    m["fnw"] = np.ascontiguousarray(np.tile(f(inputs["final_norm_w"])[None], (128, 1)))
    return m


_ROPE = None


def _partner(p):
    j = (p % 64) % 32
    return p + 16 if j < 16 else p - 16


def make_rope():
    global _ROPE
    if _ROPE is not None:
        return _ROPE
    t = np.arange(1024)
    row = (t // 64).astype(np.float32)
    col = (t % 64).astype(np.float32)
    inv = (np.float32(10000.0) ** (-np.arange(16, dtype=np.float32) / np.float32(16))).astype(np.float32)
    r = np.zeros((2, 128, 1024), np.float32)
    for p in range(128):
        d = p % 64
        half, j = d // 32, d % 32
        fq, part = j % 16, j // 16
        pos = row if half == 0 else col
        ang = (pos * inv[fq]).astype(np.float32)
        r[0, p] = np.cos(ang)
        r[1, p] = np.sin(ang) * (-1.0 if part == 0 else 1.0)
    _ROPE = r
    return r


_CST = None


def make_consts():
    global _CST
    if _CST is not None:
        return _CST
    c = np.zeros((128, C_N), np.float32)
    c[:, C_IDENT:C_IDENT + 128] = np.eye(128)
    bo = np.zeros((128, 128), np.float32)
    bo[:64, :64] = 1
    bo[64:, 64:] = 1
    c[:, C_BONES:C_BONES + 128] = bo
    p = np.arange(128)[:, None]
    q = np.arange(128)[None, :]
    fs, fi = (p < q), (p <= q)
    bs, bi = (p > q), (p >= q)
    c[:, C_MX0:C_MX0 + 512] = np.concatenate([fs, fi, fs, fi], 1)
    c[:, C_MX1:C_MX1 + 512] = np.concatenate([bs, bi, bs, bi], 1)
    c[:, C_MT0:C_MT0 + 256] = np.concatenate([q < p, q < p], 1)
    c[:, C_MT1:C_MT1 + 256] = np.concatenate([q > p, q > p], 1)
    c[:, C_TRI_GE:C_TRI_GE + 128] = (p >= q)
    c[:, C_TRI_LE:C_TRI_LE + 128] = (p <= q)
    for m_ in range(128):
        c[_partner(m_), C_PERM + m_] = 1.0
    _CST = c
    return c


def assemble(results):
    y_prompt = np.zeros((32, 256, D), np.float32)
    y_sample = np.zeros((4, 1024, D), np.float32)
    new_k = np.zeros((32, 1, 4, 256, 64), np.float32)
    new_v = np.zeros((32, 1, 4, 256, 64), np.float32)
    new_s = np.zeros((32, 1, 2, 16, 64, 64), np.float32)
    for i in range(8):
        r = results[i]
        y = np.asarray(r["y"], np.float32)
        y_prompt[4 * i:4 * i + 4] = y[:1024].reshape(4, 256, D)
        if i < 4:
            y_sample[i] = y[1024:]
        new_k[4 * i:4 * i + 4, 0] = np.asarray(r["newk"], np.float32)
        new_v[4 * i:4 * i + 4, 0] = np.asarray(r["newv"], np.float32)
        st = np.asarray(r["newsT"], np.float32).reshape(4, 2, 8, 2, 64, 64)
        new_s[4 * i:4 * i + 4, 0] = st.transpose(0, 1, 2, 3, 5, 4).reshape(4, 2, 16, 64, 64)
    return (y_prompt, y_sample, new_k, new_v, new_s)


def kernel(**inputs):
    nc, es, k = build()
    in_maps = [host_layout(inputs, i) for i in range(8)]
    res = run_bass_kernel_spmd(nc, in_maps, core_ids=list(range(8)))
    return assemble(res.results)
```
